# Optimizing a Trainium2 kernel written in Bass

```python
import math
import jax
import jax.numpy as jnp
from jax import lax
import numpy as np


D_MODEL = 1024
BATCH = 4
SEQ = 8192
DEPTH = 1

POOL_WIDTH = D_MODEL // 2
POOL_WINDOWS = (2, 4, 8, 16)
N_POOL_GROUPS = len(POOL_WINDOWS)
POOL_GROUP_DIM = POOL_WIDTH // N_POOL_GROUPS

DIFF_HEAD_DIM = 64
N_DIFF_HEADS = D_MODEL // (4 * DIFF_HEAD_DIM)
DIFF_QK_WIDTH = N_DIFF_HEADS * 2 * DIFF_HEAD_DIM
DIFF_V_WIDTH = N_DIFF_HEADS * 2 * DIFF_HEAD_DIM

IN_WIDTH = POOL_WIDTH + 2 * DIFF_QK_WIDTH + DIFF_V_WIDTH + 2 * D_MODEL
FFN_HIDDEN = int(math.ceil((8 * D_MODEL / 3) / 256) * 256)
Q_BLOCK = 128
NORM_EPS = 1e-6

kernel_name = 'hybrid_pool_diffattn_gated_block'


def rms_norm(x, g):
    xf = x.astype(jnp.float32)
    y = xf * lax.rsqrt(jnp.mean(xf * xf, axis=-1, keepdims=True) + NORM_EPS)
    return (y * g.astype(jnp.float32)).astype(x.dtype)


def alibi_slopes(n):
    start = 2.0 ** (-8.0 / n)
    return jnp.asarray(np.array([start ** (i + 1) for i in range(n)], dtype=np.float32))


def pool_mixer(p, pool_mix, pool_scale):
    B, S, _ = p.shape
    pg = p.reshape(B, S, N_POOL_GROUPS, POOL_GROUP_DIM).astype(jnp.float32)
    cs = jnp.cumsum(pg, axis=1)
    t = jnp.arange(S)
    outs = []
    for g, w in enumerate(POOL_WINDOWS):
        c = cs[:, :, g]
        prev = jnp.pad(c, ((0, 0), (w, 0), (0, 0)))[:, :S]
        cnt = jnp.minimum(t + 1, w).astype(jnp.float32)[None, :, None]
        outs.append((c - prev) / cnt - pg[:, :, g])
    y = jnp.stack(outs, axis=2).astype(p.dtype)
    y = jnp.einsum('bsgc,gcd->bsgd', y, pool_mix)
    return y.reshape(B, S, POOL_WIDTH) * pool_scale


def diff_attention(q, k, v, lam_q1, lam_k1, lam_q2, lam_k2, subln_g, lambda_init):
    B, S, _ = q.shape
    H, d = N_DIFF_HEADS, DIFF_HEAD_DIM
    q = q.reshape(B, S, H, 2, d).transpose(0, 2, 3, 1, 4)
    k = k.reshape(B, S, H, 2, d).transpose(0, 2, 3, 1, 4)
    vh = v.reshape(B, S, H, 2 * d).transpose(0, 2, 1, 3)
    k1, k2 = k[:, :, 0], k[:, :, 1]
    lam = (jnp.exp(jnp.sum(lam_q1.astype(jnp.float32) * lam_k1.astype(jnp.float32)))
           - jnp.exp(jnp.sum(lam_q2.astype(jnp.float32) * lam_k2.astype(jnp.float32)))
           + lambda_init)
    slopes = alibi_slopes(H)
    scale = 1.0 / math.sqrt(d)
    nblk = S // Q_BLOCK
    q1b = q[:, :, 0].reshape(B, H, nblk, Q_BLOCK, d).transpose(2, 0, 1, 3, 4)
    q2b = q[:, :, 1].reshape(B, H, nblk, Q_BLOCK, d).transpose(2, 0, 1, 3, 4)
    starts = jnp.arange(nblk) * Q_BLOCK
    kpos = jnp.arange(S)

    def block(args):
        qa, qb, start = args
        dist = (start + jnp.arange(Q_BLOCK))[:, None] - kpos[None, :]
        causal = dist >= 0
        bias = -slopes[:, None, None] * dist.astype(jnp.float32)[None]

        def probs(qq, kk):
            s = jnp.einsum('bhqd,bhkd->bhqk', qq, kk).astype(jnp.float32) * scale + bias
            s = jnp.where(causal, s, -jnp.inf)
            return jax.nn.softmax(s, axis=-1)

        a = probs(qa, k1) - lam * probs(qb, k2)
        return jnp.einsum('bhqk,bhkd->bhqd', a.astype(vh.dtype), vh)

    o = lax.map(block, (q1b, q2b, starts))
    o = o.transpose(1, 0, 3, 2, 4).reshape(B, S, H, 2 * d)
    o = rms_norm(o, subln_g) * (1.0 - lambda_init)
    return o.reshape(B, S, H * 2 * d)


def hybrid_layer(h, layer_idx, w_in, pool_mix, pool_scale, w_branch_a,
                 lam_q1, lam_k1, lam_q2, lam_k2, subln_g, w_branch_b, w_out,
                 mix_pre_g, mix_post_g, ffn_pre_g, ffn_post_g, w_ffn_gate, w_ffn_up, w_ffn_down):
    lambda_init = 0.8 - 0.6 * math.exp(-0.3 * layer_idx)
    u = rms_norm(h, mix_pre_g)
    z = jnp.einsum('bsd,de->bse', u, w_in)
    o1 = POOL_WIDTH
    o2 = o1 + DIFF_QK_WIDTH
    o3 = o2 + DIFF_QK_WIDTH
    o4 = o3 + DIFF_V_WIDTH
    o5 = o4 + D_MODEL
    p, q, k, v = z[..., :o1], z[..., o1:o2], z[..., o2:o3], z[..., o3:o4]
    gate_a, gate_b = z[..., o4:o5], z[..., o5:]
    ya = jnp.einsum('bsc,cd->bsd', pool_mixer(p, pool_mix, pool_scale), w_branch_a)
    yb = jnp.einsum('bsc,cd->bsd',
                    diff_attention(q, k, v, lam_q1, lam_k1, lam_q2, lam_k2, subln_g, lambda_init),
                    w_branch_b)
    m = jax.nn.sigmoid(gate_a) * ya + jax.nn.sigmoid(gate_b) * yb
    h = h + rms_norm(jnp.einsum('bsd,de->bse', m, w_out), mix_post_g)
    u = rms_norm(h, ffn_pre_g)
    f = jax.nn.silu(jnp.einsum('bsd,df->bsf', u, w_ffn_gate)) * jnp.einsum('bsd,df->bsf', u, w_ffn_up)
    h = h + rms_norm(jnp.einsum('bsf,fd->bsd', f, w_ffn_down), ffn_post_g)
    return h


def setup_inputs(seed: int = 0) -> dict:
    key = jax.random.key(seed)
    ks = jax.random.split(key, 20)
    L = DEPTH
    f32 = jnp.float32

    def nrm(k, shape, fan_in):
        return jax.random.normal(k, shape, f32) * (fan_in ** -0.5)

    def gain(k, shape):
        return 1.0 + 0.05 * jax.random.normal(k, shape, f32)

    return {
        'x': jax.random.normal(ks[0], (BATCH, SEQ, D_MODEL), f32),
        'w_in': nrm(ks[1], (L, D_MODEL, IN_WIDTH), D_MODEL),
        'pool_mix': nrm(ks[2], (L, N_POOL_GROUPS, POOL_GROUP_DIM, POOL_GROUP_DIM), POOL_GROUP_DIM),
        'pool_scale': gain(ks[3], (L, POOL_WIDTH)),
        'w_branch_a': nrm(ks[4], (L, POOL_WIDTH, D_MODEL), POOL_WIDTH),
        'lam_q1': 0.1 * jax.random.normal(ks[5], (L, DIFF_HEAD_DIM), f32),
        'lam_k1': 0.1 * jax.random.normal(ks[6], (L, DIFF_HEAD_DIM), f32),
        'lam_q2': 0.1 * jax.random.normal(ks[7], (L, DIFF_HEAD_DIM), f32),
        'lam_k2': 0.1 * jax.random.normal(ks[8], (L, DIFF_HEAD_DIM), f32),
        'subln_g': gain(ks[9], (L, 2 * DIFF_HEAD_DIM)),
        'w_branch_b': nrm(ks[10], (L, DIFF_V_WIDTH, D_MODEL), DIFF_V_WIDTH),
        'w_out': nrm(ks[11], (L, D_MODEL, D_MODEL), D_MODEL),
        'mix_pre_g': gain(ks[12], (L, D_MODEL)),
        'mix_post_g': gain(ks[13], (L, D_MODEL)),
        'ffn_pre_g': gain(ks[14], (L, D_MODEL)),
        'ffn_post_g': gain(ks[15], (L, D_MODEL)),
        'w_ffn_gate': nrm(ks[16], (L, D_MODEL, FFN_HIDDEN), D_MODEL),
        'w_ffn_up': nrm(ks[17], (L, D_MODEL, FFN_HIDDEN), D_MODEL),
        'w_ffn_down': nrm(ks[18], (L, FFN_HIDDEN, D_MODEL), FFN_HIDDEN),
    }


def reference(x, w_in, pool_mix, pool_scale, w_branch_a, lam_q1, lam_k1, lam_q2, lam_k2,
              subln_g, w_branch_b, w_out, mix_pre_g, mix_post_g, ffn_pre_g, ffn_post_g,
              w_ffn_gate, w_ffn_up, w_ffn_down):
    h = x
    for l in range(DEPTH):
        h = hybrid_layer(h, l, w_in[l], pool_mix[l], pool_scale[l], w_branch_a[l],
                         lam_q1[l], lam_k1[l], lam_q2[l], lam_k2[l], subln_g[l],
                         w_branch_b[l], w_out[l], mix_pre_g[l], mix_post_g[l],
                         ffn_pre_g[l], ffn_post_g[l], w_ffn_gate[l], w_ffn_up[l], w_ffn_down[l])
    return h
```

```python
import math
from contextlib import ExitStack

import numpy as np
import concourse.bass as bass
import concourse.mybir as mybir
from concourse.bass_utils import run_bass_kernel_spmd

F32 = mybir.dt.float32
BF16 = mybir.dt.bfloat16
AF = mybir.ActivationFunctionType
ALU = mybir.AluOpType

D_MODEL = 1024
FFN = 2816
NFC = FFN // 128
EPS = 1e-6
LAMBDA_INIT = 0.8 - 0.6 * math.exp(-0.3 * 0)
SLOPES = [0.25 ** (i + 1) for i in range(4)]
NEG_BIG = -30000.0
_STOP = [None]


class _Op:
    __slots__ = ('eng', 'fn', 'idx', 'is_dma', 'key', 'cum', 'sig', 'count', 'waits', 'clock')


class Sched:
    ENGS = ('pe', 'act', 'dve', 'pool', 'sp')

    def __init__(self, nc, es):
        self.nc = nc
        self.es = es
        self.ops = {e: [] for e in self.ENGS}
        self.last_writer = {}
        self.readers = {}
        self.dma_cum = {}
        self.seen = {e: {f: -1 for f in self.ENGS} for e in self.ENGS}
        self.seen_dma = {e: {} for e in self.ENGS}
        self.final_dmas = []

    def _add(self, eng, fn, reads, writes, is_dma, key, final=False):
        op = _Op()
        op.eng = eng
        op.fn = fn
        op.is_dma = is_dma
        op.key = key
        op.sig = False
        op.count = 0
        op.cum = 0
        reads = list(reads) + ['ARENA']
        raw = []
        other = []
        for r in reads:
            lw = self.last_writer.get(r)
            if lw is not None:
                raw.append(lw)
        for w in writes:
            lw = self.last_writer.get(w)
            if lw is not None:
                other.append(lw)
            rd = self.readers.get(w)
            if rd is not None:
                other.extend(rd[0].values())
                other.extend(rd[1].values())
        seen = self.seen[eng]
        sdma = self.seen_dma[eng]
        waits = []
        for kind, lst in ((0, raw), (1, other)):
            for d in lst:
                if d.is_dma:
                    if kind == 1 and is_dma and d.key == key:
                        continue
                    if sdma.get(d.key, 0) < d.cum:
                        waits.append(d)
                        sdma[d.key] = d.cum
                    continue
                f = d.eng
                if f == eng:
                    if eng == 'pe' or kind == 1:
                        continue
                if seen[f] < d.idx:
                    waits.append(d)
                    d.sig = True
                    ck = d.clock
                    for g in self.ENGS:
                        if ck[g] > seen[g]:
                            seen[g] = ck[g]
                    if seen[f] < d.idx:
                        seen[f] = d.idx
        op.waits = waits
        op.idx = len(self.ops[eng])
        self.ops[eng].append(op)
        if is_dma:
            cum = self.dma_cum.get(key, 0) + 16
            self.dma_cum[key] = cum
            op.cum = cum
            op.clock = None
            if final:
                self.final_dmas.append(op)
        else:
            ck = dict(seen)
            if ck[eng] < op.idx - 1:
                ck[eng] = op.idx - 1
            op.clock = ck
        for r in reads:
            rd = self.readers.get(r)
            if rd is None:
                rd = ({}, {})
                self.readers[r] = rd
            if is_dma:
                rd[1][key] = op
            else:
                rd[0][eng] = op
        for w in writes:
            self.last_writer[w] = op
            self.readers[w] = ({}, {})
        return op

    def op(self, eng, fn, reads=(), writes=()):
        return self._add(eng, fn, reads, writes, False, None)

    def dma(self, eng, fn, key, reads=(), writes=(), final=False):
        return self._add(eng, fn, reads, writes, True, key, final)

    def emit(self):
        nc = self.nc
        sems = {}
        for e in self.ENGS:
            sems[e] = self.es.enter_context(nc.semaphore("sem_" + e))
        dsems = {}
        for i, k in enumerate(self.dma_cum.keys()):
            dsems[k] = self.es.enter_context(nc.semaphore("dsem_%d" % i))
        for e in self.ENGS:
            c = 0
            for o in self.ops[e]:
                if o.sig and not o.is_dma:
                    c += 1
                    o.count = c
        fin_keys = {}
        for o in self.final_dmas:
            fin_keys[o.key] = max(fin_keys.get(o.key, 0), o.cum)

        def run(engname, eng):
            for o in self.ops[engname]:
                for d in o.waits:
                    if d.is_dma:
                        eng.wait_ge(dsems[d.key], d.cum)
                    else:
                        eng.wait_ge(sems[d.eng], d.count)
                name, kw = o.fn
                inst = getattr(eng, name)(**kw)
                if o.is_dma:
                    inst.then_inc(dsems[o.key], 16)
                elif o.sig:
                    inst.then_inc(sems[engname], 1)
            if engname == 'sp':
                for k, v in fin_keys.items():
                    eng.wait_ge(dsems[k], v)

        block = self.es.enter_context(nc.Block())
        block.sync(lambda e: run('sp', e))
        block.tensor(lambda e: run('pe', e))
        block.scalar(lambda e: run('act', e))
        block.vector(lambda e: run('dve', e))
        block.gpsimd(lambda e: run('pool', e))


def own_tiles(NT):
    t0, t1 = [], []
    for m in range(NT // 2):
        a, b = 2 * m, 2 * m + 1
        if m % 2 == 0:
            t0.append(a); t1.append(b)
        else:
            t0.append(b); t1.append(a)
    return t0, t1


def build_program(NT):
    S_LEN = NT * 512
    NOWN = NT // 2
    SOWN = NOWN * 512
    NKT = NT * 4
    nc = bass.Bass("TRN2", target_bir_lowering=False)

    def din(name, shape):
        return nc.dram_tensor(name, shape, F32, kind="ExternalInput").ap()

    xfp = din("xfp", [S_LEN, D_MODEL])
    xh = din("xh", [NOWN * 16, D_MODEL])
    ksd = din("ks", [128, NT])
    w_in = din("w_in", [D_MODEL, 4096])
    pool_mix = din("pool_mix", [4, 128, 128])
    w_a = din("w_a", [512, D_MODEL])
    w_b = din("w_b", [512, D_MODEL])
    w_out = din("w_out", [D_MODEL, D_MODEL])
    w_g = din("w_g", [D_MODEL, FFN])
    w_u = din("w_u", [D_MODEL, FFN])
    w_d = din("w_d", [FFN, D_MODEL])
    gcolsd = din("gcols", [128, 21])
    growsd = din("grows", [2, D_MODEL])
    lamd = din("lamv", [4, 64])
    outd = nc.dram_tensor("out", [SOWN, D_MODEL], F32, kind="ExternalOutput").ap()
    h1s = nc.dram_tensor("h1s", [SOWN, D_MODEL], F32, kind="ExternalOutput").ap()

    es = ExitStack()
    with es:
        def sb(name, shape, dt):
            return es.enter_context(nc.sbuf_tensor(name, shape, dt))

        S = Sched(nc, es)

        stopped = [False]

        def OP(eng, name, reads, writes, **kw):
            if not stopped[0]:
                S.op(eng, (name, kw), reads, writes)

        def DMA(eng, key, reads, writes, final=False, **kw):
            if not stopped[0]:
                S.dma(eng, ('dma_start', kw), key, reads, writes, final)

        def checkpoint(name):
            if _STOP[0] == name:
                stopped[0] = True

        ident = sb("ident", [128, 128], BF16)
        identf = sb("identf", [128, 128], F32)
        ones = sb("ones", [128, 128], BF16)
        onesf = sb("onesf", [128, 128], F32)
        gcols = sb("gcols_s", [128, 21], F32)
        gpost = sb("gpost", [128, 2, D_MODEL], F32)
        lamb = sb("lamb", [128, 4, 64], F32)
        lamt = sb("lamt", [128, 2, 64], F32)
        lams = sb("lams", [128, 8], F32)
        sublg = sb("sublg", [128, 1], F32)
        Dqp = sb("Dqp", [128, 512], F32)
        iotaq = sb("iotaq", [128, 512], F32)
        iota4 = sb("iota4", [128, 4], F32)
        iota4q = sb("iota4q", [128, 4], F32)
        masks = sb("masks", [128, 4, 512], BF16)
        ks = sb("ks_s", [128, NT], F32)
        kpos = sb("kpos", [128, NT, 4], F32)
        pen = sb("pen", [128, NOWN], F32)
        qref = sb("qref", [128, NOWN, 4], F32)
        icnt = sb("icnt", [128, 4, 16], F32)
        stat = sb("stat", [128, 64], F32)
        bar = sb("bar", [128, 8], F32)
        epsc = sb("epsc", [128, 1], F32)
        ARENA_BYTES = 184 * 1024
        arena_t = sb("arena", [128, ARENA_BYTES // 2], BF16)
        psall = es.enter_context(nc.psum_tensor("psall", [128, 8, 512], F32))

        class Arena:
            def __init__(self):
                self.off = 0

            def reset(self):
                self.off = 0

            def alloc(self, free_shape, dt, parts=128):
                n = 1
                for v in free_shape:
                    n *= v
                esz = 4 if dt == F32 else 2
                nb = n * esz
                nb_al = (nb + 63) // 64 * 64
                assert self.off + nb_al <= ARENA_BYTES, ("arena overflow", self.off, nb_al)
                a = arena_t[0:parts, self.off // 2:(self.off + nb) // 2]
                self.off += nb_al
                if dt == F32:
                    a = a.bitcast(F32)
                if len(free_shape) == 2:
                    a = a.rearrange("p (a b) -> p a b", a=free_shape[0])
                elif len(free_shape) == 3:
                    a = a.rearrange("p (a b c) -> p a b c", a=free_shape[0], b=free_shape[1])
                return a

        AR = Arena()

        def psbank(b):
            return psall[:, b, :]

        def psbank_bf(b):
            return psall[:, b, :].bitcast(BF16)

        stat_ctr = [0]

        def stat_col():
            c = stat_ctr[0] % 64
            stat_ctr[0] += 1
            return c

        def barrier():
            OP('dve', 'memset', [], ['ARENA'], ap=bar[:, 0:1], constant=0.0)

        AX = mybir.AxisListType.X

        DMA('sp', 'c0', [], ['gcols'], out=gcols[:], in_=gcolsd)
        DMA('sp', 'c1', [], ['ks'], out=ks[:], in_=ksd)
        for i in range(2):
            DMA('sp', 'c2', [], ['gpost'], out=gpost[:, i, :], in_=growsd[i:i + 1, :].partition_broadcast(128))
        for i in range(4):
            DMA('sp', 'c3', [], ['lamb'], out=lamb[:, i, :], in_=lamd[i:i + 1, :].partition_broadcast(128))
        OP('dve', 'memset', [], ['epsc'], ap=epsc[:], constant=EPS)
        OP('pool', 'memset', [], ['identf'], ap=identf[:], constant=0.0)
        OP('pool', 'affine_select', ['identf'], ['identf'], out=identf[:], in_=identf[:], pattern=[[-1, 128]],
           compare_op=ALU.not_equal, fill=1.0, base=0, channel_multiplier=1)
        OP('dve', 'tensor_copy', ['identf'], ['ident'], out=ident[:], in_=identf[:])
        OP('dve', 'memset', [], ['ones'], ap=ones[:], constant=1.0)
        OP('dve', 'memset', [], ['onesf'], ap=onesf[:], constant=1.0)
        OP('pool', 'iota', [], ['Dqp'], out=Dqp[:], pattern=[[1, 512]], base=0, channel_multiplier=-1,
           allow_small_or_imprecise_dtypes=True)
        OP('pool', 'iota', [], ['iotaq'], out=iotaq[:], pattern=[[1, 512]], base=0, channel_multiplier=0,
           allow_small_or_imprecise_dtypes=True)
        OP('pool', 'iota', [], ['iota4'], out=iota4[:], pattern=[[128, 4]], base=0, channel_multiplier=1,
           allow_small_or_imprecise_dtypes=True)
        OP('pool', 'iota', [], ['iota4q'], out=iota4q[:], pattern=[[128, 4]], base=127, channel_multiplier=0,
           allow_small_or_imprecise_dtypes=True)
        for s in range(4):
            OP('dve', 'tensor_single_scalar', ['Dqp'], ['masks'], out=masks[:, s, :], in_=Dqp[:], scalar=float(128 * s),
               op=ALU.is_ge)
        for i in range(NT):
            OP('dve', 'tensor_scalar', ['iota4', 'ks'], ['kpos'], out=kpos[:, i, :], in0=iota4[:], scalar1=ks[:, i:i + 1],
               scalar2=None, op0=ALU.add)
        for j in range(NOWN):
            OP('dve', 'tensor_tensor', ['ks'], ['pen'], out=pen[:, j:j + 1], in0=ks[:, 2 * j + 1:2 * j + 2],
               in1=ks[:, 2 * j:2 * j + 1], op=ALU.is_gt)
            OP('dve', 'tensor_scalar', ['ks', 'iota4q'], ['qref'], out=qref[:, j, :], in0=iota4q[:],
               scalar1=ks[:, 2 * j:2 * j + 1], scalar2=None, op0=ALU.add)
        OP('dve', 'tensor_scalar', ['pen'], ['pen'], out=pen[:], in0=pen[:], scalar1=NEG_BIG, scalar2=None, op0=ALU.mult)
        OP('dve', 'tensor_tensor', ['lamb'], ['lamt'], out=lamt[:, 0, :], in0=lamb[:, 0, :], in1=lamb[:, 1, :], op=ALU.mult)
        OP('dve', 'tensor_tensor', ['lamb'], ['lamt'], out=lamt[:, 1, :], in0=lamb[:, 2, :], in1=lamb[:, 3, :], op=ALU.mult)
        OP('dve', 'reduce_sum', ['lamt'], ['lams'], out=lams[:, 0:2], in_=lamt[:], axis=AX)
        OP('act', 'activation', ['lams'], ['lams2'], out=lams[:, 2:4], in_=lams[:, 0:2], func=AF.Exp)
        OP('dve', 'scalar_tensor_tensor', ['lams2'], ['neglam'], out=lams[:, 4:5], in0=lams[:, 3:4], scalar=-LAMBDA_INIT,
           in1=lams[:, 2:3], op0=ALU.add, op1=ALU.subtract)
        OP('dve', 'tensor_scalar', ['gcols'], ['sublg'], out=sublg[:], in0=gcols[:, 20:21], scalar1=1.0 - LAMBDA_INIT,
           scalar2=None, op0=ALU.mult)
        for g in range(4):
            OP('dve', 'tensor_scalar', ['iotaq', 'ks'], ['icnt'], out=icnt[:, g, :], in0=iotaq[:, 1:17], scalar1=ks[:, 0:1],
               scalar2=float(2 ** (g + 1)), op0=ALU.add, op1=ALU.min)
        OP('dve', 'reciprocal', ['icnt'], ['icnt'], out=icnt[:], in_=icnt[:])

        ctr = {'xt': 0, 'ub': 0, 'tp': 0}

        def load_norm_T(src_ap, nparts, gco, ut_dst, ut_key, XT, UB, JUNK, src_reads=()):
            b = ctr['xt'] % len(XT)
            ctr['xt'] += 1
            xt = XT[b]
            DMA('sp', ('xt', b), list(src_reads), [('xt', b)], out=xt[0:nparts, :], in_=src_ap)
            c = stat_col()
            OP('act', 'activation', [('xt', b)], ['junk', ('st', c)], out=JUNK[0:nparts, :], in_=xt[0:nparts, :],
               func=AF.Square, accum_out=stat[0:nparts, c:c + 1])
            c2 = stat_col()
            OP('act', 'activation', [('st', c), 'epsc'], [('st', c2)], out=stat[0:nparts, c2:c2 + 1],
               in_=stat[0:nparts, c:c + 1], func=AF.Sqrt, bias=epsc[0:nparts, :], scale=1.0 / D_MODEL)
            c3 = stat_col()
            OP('dve', 'reciprocal', [('st', c2)], [('st', c3)], out=stat[0:nparts, c3:c3 + 1], in_=stat[0:nparts, c2:c2 + 1])
            ub_i = ctr['ub'] % len(UB)
            ctr['ub'] += 1
            ub = UB[ub_i]
            OP('dve', 'tensor_scalar', [('xt', b), ('st', c3)], [('ub', ub_i)], out=ub[0:nparts, :], in0=xt[0:nparts, :],
               scalar1=stat[0:nparts, c3:c3 + 1], scalar2=None, op0=ALU.mult)
            tb = ctr['tp'] % 2
            ctr['tp'] += 1
            pst = psbank_bf(tb).rearrange("p (c n) -> p c n", c=8)
            for cc in range(8):
                OP('pe', 'transpose', [('ub', ub_i), 'ident'], [('ps', tb)], out=pst[:, cc, 0:nparts],
                   in_=ub[0:nparts, cc * 128:(cc + 1) * 128], identity=ident[0:nparts, 0:nparts])
            gb = gcols[:, gco:gco + 8].unsqueeze(2).to_broadcast([128, 8, nparts])
            OP('dve', 'tensor_tensor', [('ps', tb), 'gcols'], [ut_key], out=ut_dst, in0=pst[:, :, 0:nparts], in1=gb, op=ALU.mult)
            return b

        def load_w(dst, src, key, wkey, kchunks, ncols):
            step = 2048
            for k in range(kchunks):
                for c0 in range(0, ncols, step):
                    c1 = min(ncols, c0 + step)
                    DMA('pool', key, [], [wkey], out=dst[:, k, c0:c1], in_=src[k * 128:(k + 1) * 128, c0:c1])

        checkpoint('SETUP')
        for hp in range(2):
            barrier()
            AR.reset()
            OT = AR.alloc((4, SOWN), BF16)
            KT = AR.alloc((2, S_LEN), BF16)
            V = AR.alloc((NKT, 256), BF16)
            QT = AR.alloc((2, SOWN), BF16)
            Wq = AR.alloc((8, 256), BF16)
            Wk = AR.alloc((8, 256), BF16)
            Wv = AR.alloc((8, 256), BF16)
            XT = [AR.alloc((D_MODEL,), F32) for _ in range(3)]
            JUNK = AR.alloc((D_MODEL,), BF16)
            UB = [AR.alloc((D_MODEL,), BF16) for _ in range(2)]
            UT = [AR.alloc((8, 512), BF16) for _ in range(2)]
            PT = [AR.alloc((512,), BF16) for _ in range(4)]
            BIAS = [AR.alloc((4, NKT), F32) for _ in range(2)]
            R = [AR.alloc((512,), F32) for _ in range(1)]
            LACC = [AR.alloc((512,), F32) for _ in range(2)]
            A = [AR.alloc((512,), F32) for _ in range(2)]
            SQ = AR.alloc((512,), BF16)
            RT = AR.alloc((512,), F32)
            AO = AR.alloc((512,), F32)
            QBD = [AR.alloc((2, 512), BF16) for _ in range(2)]
            c_q = 512 + hp * 256
            c_k = 1024 + hp * 256
            c_v = 1536 + hp * 256
            load_w(Wk, w_in[:, c_k:c_k + 256], 'wk', 'Wk', 8, 256)
            load_w(Wv, w_in[:, c_v:c_v + 256], 'wv', 'Wv', 8, 256)
            load_w(Wq, w_in[:, c_q:c_q + 256], 'wq', 'Wq', 8, 256)

            for i in range(NT):
                if hp == 0:
                    checkpoint('A0_%d' % i)
                ub_ = i % 2
                ut = UT[ub_]
                for s in range(4):
                    r0 = (i * 4 + s) * 128
                    load_norm_T(xfp[r0:r0 + 128, :], 128, 0, ut[:, :, s * 128:(s + 1) * 128], ('ut', ub_), XT, UB, JUNK)
                for hl in range(2):
                    pb = 2 + hl
                    for c in range(8):
                        OP('pe', 'matmul', [('ut', ub_), 'Wk'], [('ps', pb)], out=psbank(pb),
                           lhsT=Wk[:, c, hl * 128:(hl + 1) * 128], rhs=ut[:, c, :], start=(c == 0), stop=(c == 7))
                    OP('act', 'activation', [('ps', pb)], ['KT'], out=KT[:, hl, i * 512:(i + 1) * 512], in_=psbank(pb),
                       func=AF.Copy)
                for s in range(4):
                    pb = 4 + (s % 2)
                    for c in range(8):
                        OP('pe', 'matmul', [('ut', ub_), 'Wv'], [('ps', pb)], out=psbank(pb)[:, 0:256],
                           lhsT=ut[:, c, s * 128:(s + 1) * 128], rhs=Wv[:, c, :], start=(c == 0), stop=(c == 7))
                    OP('dve', 'tensor_copy', [('ps', pb)], ['V'], out=V[:, i * 4 + s, :], in_=psbank(pb)[:, 0:256])
                if i % 2 == 0:
                    j = i // 2
                    for hl in range(2):
                        pb = 6 + hl
                        for c in range(8):
                            OP('pe', 'matmul', [('ut', ub_), 'Wq'], [('ps', pb)], out=psbank(pb),
                               lhsT=Wq[:, c, hl * 128:(hl + 1) * 128], rhs=ut[:, c, :], start=(c == 0), stop=(c == 7))
                        OP('act', 'activation', [('ps', pb)], ['QT'], out=QT[:, hl, j * 512:(j + 1) * 512], in_=psbank(pb),
                           func=AF.Copy)

            checkpoint('A%d' % hp)
            kpf = kpos[:].rearrange("p a b -> p (a b)")
            steps = []
            for hl in range(2):
                for j in range(NOWN):
                    nkt = 4 * (2 * j + 2)
                    lst = []
                    for kt in range(nkt):
                        for qb in range(2):
                            if kt // 4 == 2 * j and qb == 0 and kt % 4 >= 2:
                                continue
                            lst.append((kt, qb))
                    firsts = {}
                    lasts = {}
                    for idx, (kt, qb) in enumerate(lst):
                        firsts.setdefault(qb, idx)
                        lasts[qb] = idx
                    for idx, (kt, qb) in enumerate(lst):
                        steps.append((hl, j, kt, qb, nkt, idx == firsts[qb], idx == lasts[qb], idx == len(lst) - 1))
            NSTEP = len(steps)
            LA = 2
            for b2 in range(2):
                OP('pool', 'memset', [], [('qbd', b2)], ap=QBD[b2], constant=0.0)

            def emit_front(n):
                hl, j, kt, qb, nkt, first, last, glast = steps[n]
                h = 2 * hp + hl
                slope = SLOPES[h]
                gi = hl * NOWN + j
                bi = gi % 2
                bias = BIAS[bi]
                qbd = QBD[bi]
                if kt == 0 and qb == 0:
                    refs = range(4) if h == 0 else (1, 3)
                    for r in refs:
                        OP('dve', 'tensor_scalar', ['kpos', 'qref'], [('bias', bi)], out=bias[:, r, 0:nkt], in0=kpf[:, 0:nkt],
                           scalar1=qref[:, j, r:r + 1], scalar2=0.0, op0=ALU.subtract, op1=ALU.min)
                        OP('dve', 'tensor_scalar', [('bias', bi)], [('bias', bi)], out=bias[:, r, 0:nkt],
                           in0=bias[:, r, 0:nkt], scalar1=slope, scalar2=None, op0=ALU.mult)
                        OP('dve', 'tensor_scalar', [('bias', bi), 'pen'], [('bias', bi)], out=bias[:, r, nkt - 4:nkt],
                           in0=bias[:, r, nkt - 4:nkt], scalar1=pen[:, j:j + 1], scalar2=None, op0=ALU.add)
                    for q2 in range(2):
                        c0 = j * 512 + q2 * 256
                        OP('pool', 'tensor_copy', ['QT'], [('qbd', bi)], out=qbd[0:64, q2, 0:256], in_=QT[0:64, hl, c0:c0 + 256])
                        OP('pool', 'tensor_copy', ['QT'], [('qbd', bi)], out=qbd[64:128, q2, 256:512],
                           in_=QT[64:128, hl, c0:c0 + 256])
                sbank = n % 3
                pi = n % 4
                pt = PT[pi]
                OP('pe', 'matmul', ['KT', ('qbd', bi)], [('ps', sbank)], out=psbank(sbank),
                   lhsT=KT[:, hl, kt * 128:(kt + 1) * 128], rhs=qbd[:, qb, :], start=True, stop=True)
                if h != 0:
                    r = 2 * qb + 1
                    OP('act', 'activation', [('ps', sbank), ('bias', bi)], [('pt', pi)], out=pt, in_=psbank(sbank),
                       func=AF.Exp, bias=bias[:, r, kt:kt + 1], scale=0.125)
                else:
                    ptv = pt.rearrange("p (m s q) -> p m s q", m=2, s=2)
                    psv = psbank(sbank).rearrange("p (m s q) -> p m s q", m=2, s=2)
                    for sbk in range(2):
                        r = 2 * qb + sbk
                        OP('act', 'activation', [('ps', sbank), ('bias', bi)], [('pt', pi)], out=ptv[:, :, sbk, :],
                           in_=psv[:, :, sbk, :], func=AF.Exp, bias=bias[:, r, kt:kt + 1], scale=0.125)
                s4 = kt % 4
                if kt // 4 == 2 * j and ((qb == 0 and s4 < 2) or (qb == 1 and s4 >= 2)):
                    ptm = pt.rearrange("p (m q) -> p m q", m=2)
                    mk = masks[:, s4, qb * 256:(qb + 1) * 256].unsqueeze(1).to_broadcast([128, 2, 256])
                    OP('dve', 'tensor_tensor', [('pt', pi), 'masks'], [('pt', pi)], out=ptm, in0=ptm, in1=mk, op=ALU.mult)

            def emit_back(n):
                hl, j, kt, qb, nkt, first, last, glast = steps[n]
                h = 2 * hp + hl
                pi = n % 4
                pt = PT[pi]
                ob = 3 + 2 * qb
                lb = 4 + 2 * qb
                OP('pe', 'matmul', ['V', ('pt', pi)], [('ps', ob)], out=psbank(ob),
                   lhsT=V[:, kt, hl * 128:(hl + 1) * 128], rhs=pt, start=first, stop=last)
                leng = 'dve' if qb == 0 else 'pool'
                if first:
                    OP(leng, 'tensor_copy', [('pt', pi)], [('lacc', qb)], out=LACC[qb], in_=pt)
                else:
                    OP(leng, 'tensor_tensor', [('pt', pi), ('lacc', qb)], [('lacc', qb)], out=LACC[qb], in0=LACC[qb], in1=pt,
                       op=ALU.add)
                if last:
                    OP('pe', 'matmul', ['onesf', ('lacc', qb)], [('ps', lb)], out=psbank(lb), lhsT=onesf[:], rhs=LACC[qb],
                       start=True, stop=True)
                if not glast:
                    return
                for q2 in range(2):
                    OP('dve', 'reciprocal', [('ps', 4 + 2 * q2)], [('R', 0)], out=R[0], in_=psbank(4 + 2 * q2))
                    OP('dve', 'tensor_tensor', [('ps', 3 + 2 * q2), ('R', 0)], [('A', q2)], out=A[q2],
                       in0=psbank(3 + 2 * q2), in1=R[0], op=ALU.mult)
                    OP('dve', 'scalar_tensor_tensor', [('A', q2), 'neglam'], ['AO'], out=AO[:, q2 * 256:(q2 + 1) * 256],
                       in0=A[q2][:, 256:512], scalar=lams[:, 4:5], in1=A[q2][:, 0:256], op0=ALU.mult, op1=ALU.add)
                OP('act', 'activation', ['AO'], ['SQ'], out=SQ, in_=AO, func=AF.Square)
                OP('pe', 'matmul', ['ones', 'SQ'], [('ps', 7)], out=psbank(7), lhsT=ones[:], rhs=SQ, start=True, stop=True)
                OP('act', 'activation', [('ps', 7), 'epsc'], ['RT'], out=RT, in_=psbank(7), func=AF.Sqrt, bias=epsc[:],
                   scale=1.0 / 128)
                OP('dve', 'reciprocal', ['RT'], [('R', 0)], out=R[0], in_=RT)
                OP('dve', 'scalar_tensor_tensor', ['AO', ('R', 0), 'sublg'], ['OT'],
                   out=OT[:, h, j * 512:(j + 1) * 512], in0=AO, scalar=sublg[:, 0:1], in1=R[0],
                   op0=ALU.mult, op1=ALU.mult)

            for n in range(NSTEP + LA):
                if n < NSTEP:
                    emit_front(n)
                if n >= LA:
                    emit_back(n - LA)

        checkpoint('B')
        barrier()
        AR.reset()
        OT = AR.alloc((4, SOWN), BF16)
        WinC = AR.alloc((8, 2560), BF16)
        Wpm = AR.alloc((4, 128), BF16)
        Wa = AR.alloc((4, D_MODEL), BF16)
        Wb = AR.alloc((4, D_MODEL), BF16)
        Wo = AR.alloc((8, D_MODEL), BF16)
        XT = [AR.alloc((D_MODEL,), F32) for _ in range(2)]
        JUNK = AR.alloc((D_MODEL,), BF16)
        UB = [AR.alloc((D_MODEL,), BF16) for _ in range(2)]
        UTc = AR.alloc((8, 512), BF16)
        UTh = AR.alloc((8, 16), BF16)
        PP = AR.alloc((4, 528), F32)
        TA = AR.alloc((528,), F32)
        TB = AR.alloc((528,), F32)
        Y = AR.alloc((4, 512), BF16)
        YM = AR.alloc((4, 512), BF16)
        SA4 = AR.alloc((4, 512), BF16)
        SB4 = AR.alloc((4, 512), BF16)
        T1 = AR.alloc((512,), F32)
        T2 = AR.alloc((512,), F32)
        MT = AR.alloc((8, 512), BF16)
        TMP = AR.alloc((D_MODEL,), F32)
        XR = [AR.alloc((D_MODEL,), F32) for _ in range(1)]
        HO = [AR.alloc((D_MODEL,), F32) for _ in range(2)]
        for k in range(8):
            DMA('pool', 'wc', [], ['WinC'], out=WinC[:, k, 0:512], in_=w_in[k * 128:(k + 1) * 128, 0:512])
            DMA('pool', 'wc', [], ['WinC'], out=WinC[:, k, 512:2560], in_=w_in[k * 128:(k + 1) * 128, 2048:4096])
        for g in range(4):
            DMA('pool', 'wpm', [], ['Wpm'], out=Wpm[:, g, :], in_=pool_mix[g, :, :])
        load_w(Wa, w_a, 'wa', 'Wa', 4, D_MODEL)
        load_w(Wb, w_b, 'wb', 'Wb', 4, D_MODEL)
        load_w(Wo, w_out, 'wo', 'Wo', 8, D_MODEL)

        rot = [0]

        def nextbank():
            b = 2 + rot[0] % 4
            rot[0] += 1
            return b

        pair_ps = psall[:, 6:8, :].rearrange("p a b -> p (a b)")

        def post_norm_residual(res_tile, res_key, gi, dst_tile, dst_key, JUNK, TMP):
            c = stat_col()
            OP('act', 'activation', [('ps', 6), ('ps', 7)], ['junk', ('st', c)], out=JUNK, in_=pair_ps, func=AF.Square,
               accum_out=stat[:, c:c + 1])
            c2 = stat_col()
            OP('act', 'activation', [('st', c), 'epsc'], [('st', c2)], out=stat[:, c2:c2 + 1], in_=stat[:, c:c + 1],
               func=AF.Sqrt, bias=epsc[:], scale=1.0 / D_MODEL)
            c3 = stat_col()
            OP('dve', 'reciprocal', [('st', c2)], [('st', c3)], out=stat[:, c3:c3 + 1], in_=stat[:, c2:c2 + 1])
            OP('dve', 'scalar_tensor_tensor', [('ps', 6), ('ps', 7), ('st', c3), 'gpost'], ['TMP'], out=TMP, in0=pair_ps,
               scalar=stat[:, c3:c3 + 1], in1=gpost[:, gi, :], op0=ALU.mult, op1=ALU.mult)
            OP('pool', 'tensor_tensor', ['TMP', res_key], [dst_key], out=dst_tile, in0=TMP, in1=res_tile, op=ALU.add)

        for j in range(NOWN):
            for s in range(4):
                r0 = (2 * j * 4 + s) * 128
                load_norm_T(xfp[r0:r0 + 128, :], 128, 0, UTc[:, :, s * 128:(s + 1) * 128], 'utc', XT, UB, JUNK)
            load_norm_T(xh[j * 16:(j + 1) * 16, :], 16, 0, UTh[:, :, :], 'uth', XT, UB, JUNK)
            hb = nextbank()
            for g in range(4):
                for c in range(8):
                    OP('pe', 'matmul', ['WinC', 'uth'], [('ps', hb)], out=psbank(hb)[:, g * 16:(g + 1) * 16],
                       lhsT=WinC[:, c, g * 128:(g + 1) * 128], rhs=UTh[:, c, :], start=(c == 0), stop=(c == 7))
            OP('dve', 'tensor_copy', [('ps', hb)], ['PP'], out=PP[:, :, 0:16],
               in_=psbank(hb)[:, 0:64].rearrange("p (g t) -> p g t", g=4))
            for g in range(4):
                pb = nextbank()
                for c in range(8):
                    OP('pe', 'matmul', ['WinC', 'utc'], [('ps', pb)], out=psbank(pb), lhsT=WinC[:, c, g * 128:(g + 1) * 128],
                       rhs=UTc[:, c, :], start=(c == 0), stop=(c == 7))
                OP('act', 'activation', [('ps', pb)], ['PP'], out=PP[:, g, 16:528], in_=psbank(pb), func=AF.Copy)
            for g in range(4):
                w = 2 ** (g + 1)
                src = PP[:, g, :]
                src_key = 'PP'
                lo = 0
                k = 1
                tog = 0
                while k < w:
                    dst = TA if tog == 0 else TB
                    dkey = 'TA' if tog == 0 else 'TB'
                    nlo = lo + k
                    OP('pool', 'tensor_tensor', [src_key], [dkey], out=dst[:, nlo:528], in0=src[:, nlo:528],
                       in1=src[:, nlo - k:528 - k], op=ALU.add)
                    src, src_key, lo = dst, dkey, nlo
                    k *= 2
                    tog ^= 1
                OP('dve', 'scalar_tensor_tensor', [src_key, 'PP'], ['Y'], out=Y[:, g, :], in0=src[:, 16:528], scalar=1.0 / w,
                   in1=PP[:, g, 16:528], op0=ALU.mult, op1=ALU.subtract)
                if j == 0:
                    OP('pool', 'tensor_tensor', [src_key, 'icnt'], [src_key], out=src[:, 0:16], in0=src[:, 16:32],
                       in1=icnt[:, g, :], op=ALU.mult)
                    OP('pool', 'tensor_tensor', [src_key, 'PP'], ['Y'], out=Y[:, g, 0:16], in0=src[:, 0:16],
                       in1=PP[:, g, 16:32], op=ALU.subtract)
            def gate_pair(m):
                sl = m % 4
                for c in range(8):
                    OP('pe', 'matmul', ['WinC', 'utc'], [('ps', 2)], out=psbank(2),
                       lhsT=WinC[:, c, 512 + m * 128:512 + (m + 1) * 128], rhs=UTc[:, c, :], start=(c == 0), stop=(c == 7))
                OP('act', 'activation', [('ps', 2)], [('sa', sl)], out=SA4[:, sl, :], in_=psbank(2), func=AF.Sigmoid)
                for c in range(8):
                    OP('pe', 'matmul', ['WinC', 'utc'], [('ps', 3)], out=psbank(3),
                       lhsT=WinC[:, c, 1536 + m * 128:1536 + (m + 1) * 128], rhs=UTc[:, c, :], start=(c == 0), stop=(c == 7))
                OP('act', 'activation', [('ps', 3)], [('sb', sl)], out=SB4[:, sl, :], in_=psbank(3), func=AF.Sigmoid)

            def branch_pair(m):
                sl = m % 4
                for c in range(4):
                    OP('pe', 'matmul', ['Wa', 'YM'], [('ps', 4)], out=psbank(4), lhsT=Wa[:, c, m * 128:(m + 1) * 128],
                       rhs=YM[:, c, :], start=(c == 0), stop=(c == 3))
                OP('dve', 'tensor_tensor', [('ps', 4), ('sa', sl)], ['T1'], out=T1, in0=psbank(4), in1=SA4[:, sl, :], op=ALU.mult)
                for c in range(4):
                    OP('pe', 'matmul', ['Wb', 'OT'], [('ps', 5)], out=psbank(5), lhsT=Wb[:, c, m * 128:(m + 1) * 128],
                       rhs=OT[:, c, j * 512:(j + 1) * 512], start=(c == 0), stop=(c == 3))
                OP('dve', 'tensor_tensor', [('ps', 5), ('sb', sl)], ['T2'], out=T2, in0=psbank(5), in1=SB4[:, sl, :], op=ALU.mult)
                OP('pool', 'tensor_tensor', ['T1', 'T2'], ['MT'], out=MT[:, m, :], in0=T1, in1=T2, op=ALU.add)

            for m in range(4):
                gate_pair(m)
            for g in range(4):
                pb = 6 + (g % 2)
                OP('pe', 'matmul', ['Wpm', 'Y'], [('ps', pb)], out=psbank(pb), lhsT=Wpm[:, g, :], rhs=Y[:, g, :],
                   start=True, stop=True)
                OP('dve', 'tensor_scalar', [('ps', pb), 'gcols'], ['YM'], out=YM[:, g, :], in0=psbank(pb),
                   scalar1=gcols[:, 16 + g:17 + g], scalar2=None, op0=ALU.mult)
            for m in range(4):
                branch_pair(m)
            for m in range(4, 8):
                gate_pair(m)
                branch_pair(m)
            for s in range(4):
                r0 = (2 * j * 4 + s) * 128
                xi = 0
                DMA('sp', ('xr', xi), [], [('xr', xi)], out=XR[xi], in_=xfp[r0:r0 + 128, :])
                for n in range(2):
                    for c in range(8):
                        OP('pe', 'matmul', ['MT', 'Wo'], [('ps', 6 + n)], out=psbank(6 + n),
                           lhsT=MT[:, c, s * 128:(s + 1) * 128], rhs=Wo[:, c, n * 512:(n + 1) * 512],
                           start=(c == 0), stop=(c == 7))
                hi = (j * 4 + s) % 2
                post_norm_residual(XR[xi], ('xr', xi), 0, HO[hi], ('ho', hi), JUNK, TMP)
                o0 = (j * 4 + s) * 128
                DMA('sp', ('hos', hi), [('ho', hi)], [('h1s', j * 4 + s)], out=h1s[o0:o0 + 128, :], in_=HO[hi])

        checkpoint('C')
        barrier()
        AR.reset()
        Wg = AR.alloc((8, FFN), BF16)
        Wu = AR.alloc((8, FFN), BF16)
        Wd = AR.alloc((NFC, D_MODEL), BF16)
        XT = [AR.alloc((D_MODEL,), F32) for _ in range(3)]
        JUNK = AR.alloc((D_MODEL,), BF16)
        UB = [AR.alloc((D_MODEL,), BF16) for _ in range(2)]
        UTd = [AR.alloc((8, 256), BF16) for _ in range(2)]
        SG = [AR.alloc((256,), F32) for _ in range(2)]
        FT = AR.alloc((NFC, 256), BF16)
        TMP = AR.alloc((D_MODEL,), F32)
        HO = [AR.alloc((D_MODEL,), F32) for _ in range(2)]
        load_w(Wg, w_g, 'wg', 'Wg', 8, FFN)
        load_w(Wu, w_u, 'wu', 'Wu', 8, FFN)
        load_w(Wd, w_d, 'wd', 'Wd', NFC, D_MODEL)
        NT2 = SOWN // 256
        for t in range(NT2):
            utd = UTd[t % 2]
            xbs = []
            for s in range(2):
                r0 = (t * 2 + s) * 128
                xb = load_norm_T(h1s[r0:r0 + 128, :], 128, 8, utd[:, :, s * 128:(s + 1) * 128], ('utd', t % 2), XT, UB, JUNK,
                                 src_reads=[('h1s', t * 2 + s)])
                xbs.append(xb)
            for m in range(NFC):
                bg = 2 + (m % 2)
                bu = 4 + (m % 2)
                for c in range(8):
                    OP('pe', 'matmul', ['Wg', ('utd', t % 2)], [('ps', bg)], out=psbank(bg)[:, 0:256],
                       lhsT=Wg[:, c, m * 128:(m + 1) * 128], rhs=utd[:, c, :], start=(c == 0), stop=(c == 7))
                sg = SG[m % 2]
                OP('act', 'activation', [('ps', bg)], [('sg', m % 2)], out=sg, in_=psbank(bg)[:, 0:256], func=AF.Silu)
                for c in range(8):
                    OP('pe', 'matmul', ['Wu', ('utd', t % 2)], [('ps', bu)], out=psbank(bu)[:, 0:256],
                       lhsT=Wu[:, c, m * 128:(m + 1) * 128], rhs=utd[:, c, :], start=(c == 0), stop=(c == 7))
                OP('dve', 'tensor_tensor', [('ps', bu), ('sg', m % 2)], ['FT'], out=FT[:, m, :], in0=psbank(bu)[:, 0:256],
                   in1=sg, op=ALU.mult)
            for s in range(2):
                for n in range(2):
                    for m in range(NFC):
                        OP('pe', 'matmul', ['FT', 'Wd'], [('ps', 6 + n)], out=psbank(6 + n),
                           lhsT=FT[:, m, s * 128:(s + 1) * 128], rhs=Wd[:, m, n * 512:(n + 1) * 512],
                           start=(m == 0), stop=(m == NFC - 1))
                hi = (t * 2 + s) % 2
                xb = xbs[s]
                post_norm_residual(XT[xb], ('xt', xb), 1, HO[hi], ('ho', hi), JUNK, TMP)
                o0 = (t * 2 + s) * 128
                DMA('sp', ('hos', hi), [('ho', hi)], [('outd', t * 2 + s)], final=True, out=outd[o0:o0 + 128, :], in_=HO[hi])

        S.emit()
    return nc


_PROGRAM_CACHE = {}


def make_core_inputs(x, w_in, pool_mix, pool_scale, w_branch_a, lam_q1, lam_k1, lam_q2, lam_k2,
                     subln_g, w_branch_b, w_out, mix_pre_g, mix_post_g, ffn_pre_g, ffn_post_g,
                     w_ffn_gate, w_ffn_up, w_ffn_down):
    B, S_LEN, _ = x.shape
    NT = S_LEN // 512
    t0, t1 = own_tiles(NT)
    f = np.float32
    gcols = np.concatenate([
        np.asarray(mix_pre_g[0], f).reshape(8, 128).T,
        np.asarray(ffn_pre_g[0], f).reshape(8, 128).T,
        np.asarray(pool_scale[0], f).reshape(4, 128).T,
        np.asarray(subln_g[0], f).reshape(1, 128).T,
    ], axis=1)
    grows = np.stack([np.asarray(mix_post_g[0], f), np.asarray(ffn_post_g[0], f)], axis=0)
    lamv = np.stack([np.asarray(v[0], f) for v in (lam_q1, lam_k1, lam_q2, lam_k2)], axis=0)
    shared = {
        "w_in": np.ascontiguousarray(w_in[0], f), "pool_mix": np.ascontiguousarray(pool_mix[0], f),
        "w_a": np.ascontiguousarray(w_branch_a[0], f), "w_b": np.ascontiguousarray(w_branch_b[0], f),
        "w_out": np.ascontiguousarray(w_out[0], f), "w_g": np.ascontiguousarray(w_ffn_gate[0], f),
        "w_u": np.ascontiguousarray(w_ffn_up[0], f), "w_d": np.ascontiguousarray(w_ffn_down[0], f),
        "gcols": np.ascontiguousarray(gcols), "grows": np.ascontiguousarray(grows), "lamv": np.ascontiguousarray(lamv),
    }
    in_maps = []
    orders = []
    for core in range(2 * B):
        b, r = core // 2, core % 2
        own = t0 if r == 0 else t1
        oth = t1 if r == 0 else t0
        order = []
        for m in range(NT // 2):
            order += [own[m], oth[m]]
        xb = np.asarray(x[b], f).reshape(NT, 512, D_MODEL)
        xfp = np.ascontiguousarray(xb[order].reshape(S_LEN, D_MODEL))
        xhalo = np.zeros((NT // 2, 16, D_MODEL), f)
        for jj, t in enumerate(own):
            if t > 0:
                xhalo[jj] = x[b, t * 512 - 16:t * 512]
        ksv = np.tile(np.asarray([512.0 * t for t in order], f)[None, :], (128, 1))
        m = dict(shared)
        m.update({"xfp": xfp, "xh": np.ascontiguousarray(xhalo.reshape(-1, D_MODEL)), "ks": np.ascontiguousarray(ksv)})
        in_maps.append(m)
        orders.append(own)
    return in_maps, orders


def kernel(**inputs):
    inputs = {k: np.asarray(v) for k, v in inputs.items()}
    x = inputs["x"]
    B, S_LEN, _ = x.shape
    NT = S_LEN // 512
    if NT not in _PROGRAM_CACHE:
        _PROGRAM_CACHE[NT] = build_program(NT)
    nc = _PROGRAM_CACHE[NT]
    in_maps, orders = make_core_inputs(**inputs)
    res = run_bass_kernel_spmd(nc, in_maps, core_ids=list(range(2 * B)))
    out = np.empty((B, S_LEN, D_MODEL), np.float32)
    for core in range(2 * B):
        b = core // 2
        o = np.asarray(res.results[core]["out"]).reshape(NT // 2, 512, D_MODEL)
        for jj, t in enumerate(orders[core]):
            out[b, t * 512:(t + 1) * 512] = o[jj]
    return out
```

```python
import math
from contextlib import ExitStack

import numpy as np
import concourse.bass as bass
import concourse.mybir as mybir
from concourse.bass_utils import run_bass_kernel_spmd

F32 = mybir.dt.float32
BF16 = mybir.dt.bfloat16
AF = mybir.ActivationFunctionType
ALU = mybir.AluOpType

D_MODEL = 1024
FFN = 2816
NFC = FFN // 128
EPS = 1e-6
LAMBDA_INIT = 0.8 - 0.6 * math.exp(-0.3 * 0)
SLOPES = [0.25 ** (i + 1) for i in range(4)]
NEG_BIG = -30000.0
_STOP = [None]


class _Op:
    __slots__ = ('eng', 'fn', 'idx', 'is_dma', 'key', 'cum', 'sig', 'count', 'waits', 'clock')


class Sched:
    ENGS = ('pe', 'act', 'dve', 'pool', 'sp')

    def __init__(self, nc, es):
        self.nc = nc
        self.es = es
        self.ops = {e: [] for e in self.ENGS}
        self.last_writer = {}
        self.readers = {}
        self.dma_cum = {}
        self.seen = {e: {f: -1 for f in self.ENGS} for e in self.ENGS}
        self.seen_dma = {e: {} for e in self.ENGS}
        self.final_dmas = []

    def _add(self, eng, fn, reads, writes, is_dma, key, final=False):
        op = _Op()
        op.eng = eng
        op.fn = fn
        op.is_dma = is_dma
        op.key = key
        op.sig = False
        op.count = 0
        op.cum = 0
        reads = list(reads) + ['ARENA']
        raw = []
        other = []
        for r in reads:
            lw = self.last_writer.get(r)
            if lw is not None:
                raw.append(lw)
        for w in writes:
            lw = self.last_writer.get(w)
            if lw is not None:
                other.append(lw)
            rd = self.readers.get(w)
            if rd is not None:
                other.extend(rd[0].values())
                other.extend(rd[1].values())
        seen = self.seen[eng]
        sdma = self.seen_dma[eng]
        waits = []
        for kind, lst in ((0, raw), (1, other)):
            for d in lst:
                if d.is_dma:
                    if kind == 1 and is_dma and d.key == key:
                        continue
                    if sdma.get(d.key, 0) < d.cum:
                        waits.append(d)
                        sdma[d.key] = d.cum
                    continue
                f = d.eng
                if f == eng:
                    if eng == 'pe' or kind == 1:
                        continue
                if seen[f] < d.idx:
                    waits.append(d)
                    d.sig = True
                    ck = d.clock
                    for g in self.ENGS:
                        if ck[g] > seen[g]:
                            seen[g] = ck[g]
                    if seen[f] < d.idx:
                        seen[f] = d.idx
        op.waits = waits
        op.idx = len(self.ops[eng])
        self.ops[eng].append(op)
        if is_dma:
            cum = self.dma_cum.get(key, 0) + 16
            self.dma_cum[key] = cum
            op.cum = cum
            op.clock = None
            if final:
                self.final_dmas.append(op)
        else:
            ck = dict(seen)
            if ck[eng] < op.idx - 1:
                ck[eng] = op.idx - 1
            op.clock = ck
        for r in reads:
            rd = self.readers.get(r)
            if rd is None:
                rd = ({}, {})
                self.readers[r] = rd
            if is_dma:
                rd[1][key] = op
            else:
                rd[0][eng] = op
        for w in writes:
            self.last_writer[w] = op
            self.readers[w] = ({}, {})
        return op

    def op(self, eng, fn, reads=(), writes=()):
        return self._add(eng, fn, reads, writes, False, None)

    def dma(self, eng, fn, key, reads=(), writes=(), final=False):
        return self._add(eng, fn, reads, writes, True, key, final)

    def emit(self):
        nc = self.nc
        sems = {}
        for e in self.ENGS:
            sems[e] = self.es.enter_context(nc.semaphore("sem_" + e))
        dsems = {}
        for i, k in enumerate(self.dma_cum.keys()):
            dsems[k] = self.es.enter_context(nc.semaphore("dsem_%d" % i))
        for e in self.ENGS:
            c = 0
            for o in self.ops[e]:
                if o.sig and not o.is_dma:
                    c += 1
                    o.count = c
        fin_keys = {}
        for o in self.final_dmas:
            fin_keys[o.key] = max(fin_keys.get(o.key, 0), o.cum)

        def run(engname, eng):
            for o in self.ops[engname]:
                for d in o.waits:
                    if d.is_dma:
                        eng.wait_ge(dsems[d.key], d.cum)
                    else:
                        eng.wait_ge(sems[d.eng], d.count)
                name, kw = o.fn
                inst = getattr(eng, name)(**kw)
                if o.is_dma:
                    inst.then_inc(dsems[o.key], 16)
                elif o.sig:
                    inst.then_inc(sems[engname], 1)
            if engname == 'sp':
                for k, v in fin_keys.items():
                    eng.wait_ge(dsems[k], v)

        block = self.es.enter_context(nc.Block())
        block.sync(lambda e: run('sp', e))
        block.tensor(lambda e: run('pe', e))
        block.scalar(lambda e: run('act', e))
        block.vector(lambda e: run('dve', e))
        block.gpsimd(lambda e: run('pool', e))


def own_tiles(NT):
    t0, t1 = [], []
    for m in range(NT // 2):
        a, b = 2 * m, 2 * m + 1
        if m % 2 == 0:
            t0.append(a); t1.append(b)
        else:
            t0.append(b); t1.append(a)
    return t0, t1


def build_program(NT):
    S_LEN = NT * 512
    NOWN = NT // 2
    SOWN = NOWN * 512
    NKT = NT * 4
    nc = bass.Bass("TRN2", target_bir_lowering=False)

    def din(name, shape):
        return nc.dram_tensor(name, shape, F32, kind="ExternalInput").ap()

    xfp = din("xfp", [S_LEN, D_MODEL])
    xh = din("xh", [NOWN * 16, D_MODEL])
    ksd = din("ks", [128, NT])
    w_in = din("w_in", [D_MODEL, 4096])
    pool_mix = din("pool_mix", [4, 128, 128])
    w_a = din("w_a", [512, D_MODEL])
    w_b = din("w_b", [512, D_MODEL])
    w_out = din("w_out", [D_MODEL, D_MODEL])
    w_g = din("w_g", [D_MODEL, FFN])
    w_u = din("w_u", [D_MODEL, FFN])
    w_d = din("w_d", [FFN, D_MODEL])
    gcolsd = din("gcols", [128, 21])
    growsd = din("grows", [2, D_MODEL])
    lamd = din("lamv", [4, 64])
    outd = nc.dram_tensor("out", [SOWN, D_MODEL], F32, kind="ExternalOutput").ap()
    h1s = nc.dram_tensor("h1s", [SOWN, D_MODEL], F32, kind="ExternalOutput").ap()

    es = ExitStack()
    with es:
        def sb(name, shape, dt):
            return es.enter_context(nc.sbuf_tensor(name, shape, dt))

        S = Sched(nc, es)

        stopped = [False]

        def OP(eng, name, reads, writes, **kw):
            if not stopped[0]:
                S.op(eng, (name, kw), reads, writes)

        def DMA(eng, key, reads, writes, final=False, **kw):
            if not stopped[0]:
                S.dma(eng, ('dma_start', kw), key, reads, writes, final)

        def checkpoint(name):
            if _STOP[0] == name:
                stopped[0] = True

        ident = sb("ident", [128, 128], BF16)
        identf = sb("identf", [128, 128], F32)
        ones = sb("ones", [128, 128], BF16)
        onesf = sb("onesf", [128, 128], F32)
        gcols = sb("gcols_s", [128, 21], F32)
        gpost = sb("gpost", [128, 2, D_MODEL], F32)
        lamb = sb("lamb", [128, 4, 64], F32)
        lamt = sb("lamt", [128, 2, 64], F32)
        lams = sb("lams", [128, 8], F32)
        sublg = sb("sublg", [128, 1], F32)
        Dqp = sb("Dqp", [128, 512], F32)
        iotaq = sb("iotaq", [128, 512], F32)
        iota4 = sb("iota4", [128, 4], F32)
        iota4q = sb("iota4q", [128, 4], F32)
        masks = sb("masks", [128, 4, 512], BF16)
        ks = sb("ks_s", [128, NT], F32)
        kpos = sb("kpos", [128, NT, 4], F32)
        pen = sb("pen", [128, NOWN], F32)
        qref = sb("qref", [128, NOWN, 4], F32)
        icnt = sb("icnt", [128, 4, 16], F32)
        stat = sb("stat", [128, 64], F32)
        bar = sb("bar", [128, 8], F32)
        epsc = sb("epsc", [128, 1], F32)
        ARENA_BYTES = 184 * 1024
        arena_t = sb("arena", [128, ARENA_BYTES // 2], BF16)
        psall = es.enter_context(nc.psum_tensor("psall", [128, 8, 512], F32))

        class Arena:
            def __init__(self):
                self.off = 0

            def reset(self):
                self.off = 0

            def alloc(self, free_shape, dt, parts=128):
                n = 1
                for v in free_shape:
                    n *= v
                esz = 4 if dt == F32 else 2
                nb = n * esz
                nb_al = (nb + 63) // 64 * 64
                assert self.off + nb_al <= ARENA_BYTES, ("arena overflow", self.off, nb_al)
                a = arena_t[0:parts, self.off // 2:(self.off + nb) // 2]
                self.off += nb_al
                if dt == F32:
                    a = a.bitcast(F32)
                if len(free_shape) == 2:
                    a = a.rearrange("p (a b) -> p a b", a=free_shape[0])
                elif len(free_shape) == 3:
                    a = a.rearrange("p (a b c) -> p a b c", a=free_shape[0], b=free_shape[1])
                return a

        AR = Arena()

        def psbank(b):
            return psall[:, b, :]

        def psbank_bf(b):
            return psall[:, b, :].bitcast(BF16)

        stat_ctr = [0]

        def stat_col():
            c = stat_ctr[0] % 64
            stat_ctr[0] += 1
            return c

        def barrier():
            OP('dve', 'memset', [], ['ARENA'], ap=bar[:, 0:1], constant=0.0)

        AX = mybir.AxisListType.X

        DMA('sp', 'c0', [], ['gcols'], out=gcols[:], in_=gcolsd)
        DMA('sp', 'c1', [], ['ks'], out=ks[:], in_=ksd)
        for i in range(2):
            DMA('sp', 'c2', [], ['gpost'], out=gpost[:, i, :], in_=growsd[i:i + 1, :].partition_broadcast(128))
        for i in range(4):
            DMA('sp', 'c3', [], ['lamb'], out=lamb[:, i, :], in_=lamd[i:i + 1, :].partition_broadcast(128))
        OP('dve', 'memset', [], ['epsc'], ap=epsc[:], constant=EPS)
        OP('pool', 'memset', [], ['identf'], ap=identf[:], constant=0.0)
        OP('pool', 'affine_select', ['identf'], ['identf'], out=identf[:], in_=identf[:], pattern=[[-1, 128]],
           compare_op=ALU.not_equal, fill=1.0, base=0, channel_multiplier=1)
        OP('dve', 'tensor_copy', ['identf'], ['ident'], out=ident[:], in_=identf[:])
        OP('dve', 'memset', [], ['ones'], ap=ones[:], constant=1.0)
        OP('dve', 'memset', [], ['onesf'], ap=onesf[:], constant=1.0)
        OP('pool', 'iota', [], ['Dqp'], out=Dqp[:], pattern=[[1, 512]], base=0, channel_multiplier=-1,
           allow_small_or_imprecise_dtypes=True)
        OP('pool', 'iota', [], ['iotaq'], out=iotaq[:], pattern=[[1, 512]], base=0, channel_multiplier=0,
           allow_small_or_imprecise_dtypes=True)
        OP('pool', 'iota', [], ['iota4'], out=iota4[:], pattern=[[128, 4]], base=0, channel_multiplier=1,
           allow_small_or_imprecise_dtypes=True)
        OP('pool', 'iota', [], ['iota4q'], out=iota4q[:], pattern=[[128, 4]], base=127, channel_multiplier=0,
           allow_small_or_imprecise_dtypes=True)
        for s in range(4):
            OP('dve', 'tensor_single_scalar', ['Dqp'], ['masks'], out=masks[:, s, :], in_=Dqp[:], scalar=float(128 * s),
               op=ALU.is_ge)
        for i in range(NT):
            OP('dve', 'tensor_scalar', ['iota4', 'ks'], ['kpos'], out=kpos[:, i, :], in0=iota4[:], scalar1=ks[:, i:i + 1],
               scalar2=None, op0=ALU.add)
        for j in range(NOWN):
            OP('dve', 'tensor_tensor', ['ks'], ['pen'], out=pen[:, j:j + 1], in0=ks[:, 2 * j + 1:2 * j + 2],
               in1=ks[:, 2 * j:2 * j + 1], op=ALU.is_gt)
            OP('dve', 'tensor_scalar', ['ks', 'iota4q'], ['qref'], out=qref[:, j, :], in0=iota4q[:],
               scalar1=ks[:, 2 * j:2 * j + 1], scalar2=None, op0=ALU.add)
        OP('dve', 'tensor_scalar', ['pen'], ['pen'], out=pen[:], in0=pen[:], scalar1=NEG_BIG, scalar2=None, op0=ALU.mult)
        OP('dve', 'tensor_tensor', ['lamb'], ['lamt'], out=lamt[:, 0, :], in0=lamb[:, 0, :], in1=lamb[:, 1, :], op=ALU.mult)
        OP('dve', 'tensor_tensor', ['lamb'], ['lamt'], out=lamt[:, 1, :], in0=lamb[:, 2, :], in1=lamb[:, 3, :], op=ALU.mult)
        OP('dve', 'reduce_sum', ['lamt'], ['lams'], out=lams[:, 0:2], in_=lamt[:], axis=AX)
        OP('act', 'activation', ['lams'], ['lams2'], out=lams[:, 2:4], in_=lams[:, 0:2], func=AF.Exp)
        OP('dve', 'scalar_tensor_tensor', ['lams2'], ['neglam'], out=lams[:, 4:5], in0=lams[:, 3:4], scalar=-LAMBDA_INIT,
           in1=lams[:, 2:3], op0=ALU.add, op1=ALU.subtract)
        OP('dve', 'tensor_scalar', ['gcols'], ['sublg'], out=sublg[:], in0=gcols[:, 20:21], scalar1=1.0 - LAMBDA_INIT,
           scalar2=None, op0=ALU.mult)
        for g in range(4):
            OP('dve', 'tensor_scalar', ['iotaq', 'ks'], ['icnt'], out=icnt[:, g, :], in0=iotaq[:, 1:17], scalar1=ks[:, 0:1],
               scalar2=float(2 ** (g + 1)), op0=ALU.add, op1=ALU.min)
        OP('dve', 'reciprocal', ['icnt'], ['icnt'], out=icnt[:], in_=icnt[:])

        ctr = {'xt': 0, 'ub': 0, 'tp': 0}

        def prep_a(src_ap, nparts, XT, UB, JUNK, src_reads=()):
            b = ctr['xt'] % len(XT)
            ctr['xt'] += 1
            xt = XT[b]
            DMA('sp', ('xt', b), list(src_reads), [('xt', b)], out=xt[0:nparts, :], in_=src_ap)
            c = stat_col()
            OP('act', 'activation', [('xt', b)], ['junk', ('st', c)], out=JUNK[0:nparts, :], in_=xt[0:nparts, :],
               func=AF.Square, accum_out=stat[0:nparts, c:c + 1])
            c2 = stat_col()
            OP('act', 'activation', [('st', c), 'epsc'], [('st', c2)], out=stat[0:nparts, c2:c2 + 1],
               in_=stat[0:nparts, c:c + 1], func=AF.Sqrt, bias=epsc[0:nparts, :], scale=1.0 / D_MODEL)
            c3 = stat_col()
            OP('dve', 'reciprocal', [('st', c2)], [('st', c3)], out=stat[0:nparts, c3:c3 + 1], in_=stat[0:nparts, c2:c2 + 1])
            ub_i = ctr['ub'] % len(UB)
            ctr['ub'] += 1
            ub = UB[ub_i]
            OP('dve', 'tensor_scalar', [('xt', b), ('st', c3)], [('ub', ub_i)], out=ub[0:nparts, :], in0=xt[0:nparts, :],
               scalar1=stat[0:nparts, c3:c3 + 1], scalar2=None, op0=ALU.mult)
            return b, ub_i

        def prep_b(ub_i, nparts, gco, ut_dst, ut_key, UB):
            ub = UB[ub_i]
            tb = ctr['tp'] % 2
            ctr['tp'] += 1
            pst = psbank_bf(tb).rearrange("p (c n) -> p c n", c=8)
            for cc in range(8):
                OP('pe', 'transpose', [('ub', ub_i), 'ident'], [('ps', tb)], out=pst[:, cc, 0:nparts],
                   in_=ub[0:nparts, cc * 128:(cc + 1) * 128], identity=ident[0:nparts, 0:nparts])
            gb = gcols[:, gco:gco + 8].unsqueeze(2).to_broadcast([128, 8, nparts])
            OP('dve', 'tensor_tensor', [('ps', tb), 'gcols'], [ut_key], out=ut_dst, in0=pst[:, :, 0:nparts], in1=gb, op=ALU.mult)

        def load_norm_T(src_ap, nparts, gco, ut_dst, ut_key, XT, UB, JUNK, src_reads=()):
            b, ub_i = prep_a(src_ap, nparts, XT, UB, JUNK, src_reads)
            prep_b(ub_i, nparts, gco, ut_dst, ut_key, UB)
            return b

        def load_w(dst, src, key, wkey, kchunks, ncols):
            step = 2048
            for k in range(kchunks):
                for c0 in range(0, ncols, step):
                    c1 = min(ncols, c0 + step)
                    DMA('pool', key, [], [wkey], out=dst[:, k, c0:c1], in_=src[k * 128:(k + 1) * 128, c0:c1])

        checkpoint('SETUP')
        for hp in range(2):
            barrier()
            AR.reset()
            OT = AR.alloc((4, SOWN), BF16)
            KT = AR.alloc((2, S_LEN), BF16)
            V = AR.alloc((NKT, 256), BF16)
            QT = AR.alloc((2, SOWN), BF16)
            Wq = AR.alloc((8, 256), BF16)
            Wk = AR.alloc((8, 256), BF16)
            Wv = AR.alloc((8, 256), BF16)
            XT = [AR.alloc((D_MODEL,), F32) for _ in range(3)]
            JUNK = AR.alloc((D_MODEL,), BF16)
            UB = [AR.alloc((D_MODEL,), BF16) for _ in range(4)]
            UT = [AR.alloc((8, 512), BF16) for _ in range(2)]
            PT = [AR.alloc((512,), BF16) for _ in range(4)]
            BIAS = [AR.alloc((4, NKT), F32) for _ in range(2)]
            R = [AR.alloc((512,), F32) for _ in range(1)]
            A = [AR.alloc((512,), F32) for _ in range(2)]
            SQ = AR.alloc((512,), BF16)
            RT = AR.alloc((512,), F32)
            AO = AR.alloc((512,), F32)
            QBD = [AR.alloc((2, 512), BF16) for _ in range(2)]
            c_q = 512 + hp * 256
            c_k = 1024 + hp * 256
            c_v = 1536 + hp * 256
            load_w(Wk, w_in[:, c_k:c_k + 256], 'wk', 'Wk', 8, 256)
            load_w(Wv, w_in[:, c_v:c_v + 256], 'wv', 'Wv', 8, 256)
            load_w(Wq, w_in[:, c_q:c_q + 256], 'wq', 'Wq', 8, 256)

            slot_ub = {}

            def prepA(i):
                slot_ub[i] = []
                for s in range(4):
                    r0 = (i * 4 + s) * 128
                    slot_ub[i].append(prep_a(xfp[r0:r0 + 128, :], 128, XT, UB, JUNK)[1])

            def prepB(i):
                for s in range(4):
                    prep_b(slot_ub[i][s], 128, 0, UT[i % 2][:, :, s * 128:(s + 1) * 128], ('ut', i % 2), UB)

            prepA(0)
            prepB(0)
            for i in range(NT):
                ub_ = i % 2
                ut = UT[ub_]
                if i + 1 < NT:
                    prepA(i + 1)
                for hl in range(2):
                    pb = 2 + hl
                    for c in range(8):
                        OP('pe', 'matmul', [('ut', ub_), 'Wk'], [('ps', pb)], out=psbank(pb),
                           lhsT=Wk[:, c, hl * 128:(hl + 1) * 128], rhs=ut[:, c, :], start=(c == 0), stop=(c == 7))
                    OP('act', 'activation', [('ps', pb)], ['KT'], out=KT[:, hl, i * 512:(i + 1) * 512], in_=psbank(pb),
                       func=AF.Copy)
                for s in range(4):
                    pb = 4 + (s % 2)
                    for c in range(8):
                        OP('pe', 'matmul', [('ut', ub_), 'Wv'], [('ps', pb)], out=psbank(pb)[:, 0:256],
                           lhsT=ut[:, c, s * 128:(s + 1) * 128], rhs=Wv[:, c, :], start=(c == 0), stop=(c == 7))
                    OP('dve', 'tensor_copy', [('ps', pb)], ['V'], out=V[:, i * 4 + s, :], in_=psbank(pb)[:, 0:256])
                if i % 2 == 0:
                    j = i // 2
                    for hl in range(2):
                        pb = 6 + hl
                        for c in range(8):
                            OP('pe', 'matmul', [('ut', ub_), 'Wq'], [('ps', pb)], out=psbank(pb),
                               lhsT=Wq[:, c, hl * 128:(hl + 1) * 128], rhs=ut[:, c, :], start=(c == 0), stop=(c == 7))
                        OP('act', 'activation', [('ps', pb)], ['QT'], out=QT[:, hl, j * 512:(j + 1) * 512], in_=psbank(pb),
                           func=AF.Copy)
                if i + 1 < NT:
                    prepB(i + 1)

            checkpoint('A%d' % hp)
            kpf = kpos[:].rearrange("p a b -> p (a b)")
            steps = []
            for hl in range(2):
                for j in range(NOWN):
                    nkt = 4 * (2 * j + 2)
                    lst = []
                    for kt in range(nkt):
                        for qb in range(2):
                            if kt // 4 == 2 * j and qb == 0 and kt % 4 >= 2:
                                continue
                            lst.append((kt, qb))
                    firsts = {}
                    lasts = {}
                    for idx, (kt, qb) in enumerate(lst):
                        firsts.setdefault(qb, idx)
                        lasts[qb] = idx
                    for idx, (kt, qb) in enumerate(lst):
                        steps.append((hl, j, kt, qb, nkt, idx == firsts[qb], idx == lasts[qb], idx == len(lst) - 1))
            NSTEP = len(steps)
            LA = 2
            for b2 in range(2):
                OP('pool', 'memset', [], [('qbd', b2)], ap=QBD[b2], constant=0.0)

            def emit_front(n):
                hl, j, kt, qb, nkt, first, last, glast = steps[n]
                h = 2 * hp + hl
                slope = SLOPES[h]
                gi = hl * NOWN + j
                bi = gi % 2
                bias = BIAS[bi]
                qbd = QBD[bi]
                if kt == 0 and qb == 0:
                    refs = range(4) if h == 0 else (1, 3)
                    for r in refs:
                        OP('dve', 'tensor_scalar', ['kpos', 'qref'], [('bias', bi)], out=bias[:, r, 0:nkt], in0=kpf[:, 0:nkt],
                           scalar1=qref[:, j, r:r + 1], scalar2=0.0, op0=ALU.subtract, op1=ALU.min)
                        OP('dve', 'tensor_scalar', [('bias', bi)], [('bias', bi)], out=bias[:, r, 0:nkt],
                           in0=bias[:, r, 0:nkt], scalar1=slope, scalar2=None, op0=ALU.mult)
                        OP('dve', 'tensor_scalar', [('bias', bi), 'pen'], [('bias', bi)], out=bias[:, r, nkt - 4:nkt],
                           in0=bias[:, r, nkt - 4:nkt], scalar1=pen[:, j:j + 1], scalar2=None, op0=ALU.add)
                    for q2 in range(2):
                        c0 = j * 512 + q2 * 256
                        OP('pool', 'tensor_copy', ['QT'], [('qbd', bi)], out=qbd[0:64, q2, 0:256], in_=QT[0:64, hl, c0:c0 + 256])
                        OP('pool', 'tensor_copy', ['QT'], [('qbd', bi)], out=qbd[64:128, q2, 256:512],
                           in_=QT[64:128, hl, c0:c0 + 256])
                sbank = n % 3
                pi = n % 4
                pt = PT[pi]
                OP('pe', 'matmul', ['KT', ('qbd', bi)], [('ps', sbank)], out=psbank(sbank),
                   lhsT=KT[:, hl, kt * 128:(kt + 1) * 128], rhs=qbd[:, qb, :], start=True, stop=True)
                if h != 0:
                    r = 2 * qb + 1
                    OP('act', 'activation', [('ps', sbank), ('bias', bi)], [('pt', pi)], out=pt, in_=psbank(sbank),
                       func=AF.Exp, bias=bias[:, r, kt:kt + 1], scale=0.125)
                else:
                    ptv = pt.rearrange("p (m s q) -> p m s q", m=2, s=2)
                    psv = psbank(sbank).rearrange("p (m s q) -> p m s q", m=2, s=2)
                    for sbk in range(2):
                        r = 2 * qb + sbk
                        OP('act', 'activation', [('ps', sbank), ('bias', bi)], [('pt', pi)], out=ptv[:, :, sbk, :],
                           in_=psv[:, :, sbk, :], func=AF.Exp, bias=bias[:, r, kt:kt + 1], scale=0.125)
                s4 = kt % 4
                if kt // 4 == 2 * j and ((qb == 0 and s4 < 2) or (qb == 1 and s4 >= 2)):
                    ptm = pt.rearrange("p (m q) -> p m q", m=2)
                    mk = masks[:, s4, qb * 256:(qb + 1) * 256].unsqueeze(1).to_broadcast([128, 2, 256])
                    OP('dve', 'tensor_tensor', [('pt', pi), 'masks'], [('pt', pi)], out=ptm, in0=ptm, in1=mk, op=ALU.mult)

            def emit_back(n):
                hl, j, kt, qb, nkt, first, last, glast = steps[n]
                h = 2 * hp + hl
                pi = n % 4
                pt = PT[pi]
                ob = 3 + 2 * qb
                lb = 4 + 2 * qb
                OP('pe', 'matmul', ['V', ('pt', pi)], [('ps', ob)], out=psbank(ob),
                   lhsT=V[:, kt, hl * 128:(hl + 1) * 128], rhs=pt, start=first, stop=last)
                OP('pe', 'matmul', ['ones', ('pt', pi)], [('ps', lb)], out=psbank(lb), lhsT=ones[:], rhs=pt,
                   start=first, stop=last)
                if not glast:
                    return
                for q2 in range(2):
                    OP('dve', 'reciprocal', [('ps', 4 + 2 * q2)], [('R', 0)], out=R[0], in_=psbank(4 + 2 * q2))
                    OP('dve', 'tensor_tensor', [('ps', 3 + 2 * q2), ('R', 0)], [('A', q2)], out=A[q2],
                       in0=psbank(3 + 2 * q2), in1=R[0], op=ALU.mult)
                    OP('dve', 'scalar_tensor_tensor', [('A', q2), 'neglam'], ['AO'], out=AO[:, q2 * 256:(q2 + 1) * 256],
                       in0=A[q2][:, 256:512], scalar=lams[:, 4:5], in1=A[q2][:, 0:256], op0=ALU.mult, op1=ALU.add)
                OP('act', 'activation', ['AO'], ['SQ'], out=SQ, in_=AO, func=AF.Square)
                OP('pe', 'matmul', ['ones', 'SQ'], [('ps', 7)], out=psbank(7), lhsT=ones[:], rhs=SQ, start=True, stop=True)
                OP('act', 'activation', [('ps', 7), 'epsc'], ['RT'], out=RT, in_=psbank(7), func=AF.Sqrt, bias=epsc[:],
                   scale=1.0 / 128)
                OP('dve', 'reciprocal', ['RT'], [('R', 0)], out=R[0], in_=RT)
                OP('dve', 'scalar_tensor_tensor', ['AO', ('R', 0), 'sublg'], ['OT'],
                   out=OT[:, h, j * 512:(j + 1) * 512], in0=AO, scalar=sublg[:, 0:1], in1=R[0],
                   op0=ALU.mult, op1=ALU.mult)

            for n in range(NSTEP + LA):
                if n < NSTEP:
                    emit_front(n)
                if n >= LA:
                    emit_back(n - LA)

        checkpoint('B')
        barrier()
        AR.reset()
        OT = AR.alloc((4, SOWN), BF16)
        WinC = AR.alloc((8, 2560), BF16)
        Wpm = AR.alloc((4, 128), BF16)
        Wa = AR.alloc((4, D_MODEL), BF16)
        Wb = AR.alloc((4, D_MODEL), BF16)
        Wo = AR.alloc((8, D_MODEL), BF16)
        XT = [AR.alloc((D_MODEL,), F32) for _ in range(2)]
        JUNK = AR.alloc((D_MODEL,), BF16)
        UB = [AR.alloc((D_MODEL,), BF16) for _ in range(2)]
        UTc = AR.alloc((8, 512), BF16)
        UTh = AR.alloc((8, 16), BF16)
        PP = AR.alloc((4, 528), F32)
        TA = AR.alloc((528,), F32)
        TB = AR.alloc((528,), F32)
        Y = AR.alloc((4, 512), BF16)
        YM = AR.alloc((4, 512), BF16)
        SA4 = AR.alloc((4, 512), BF16)
        SB4 = AR.alloc((4, 512), BF16)
        T1 = AR.alloc((512,), F32)
        T2 = AR.alloc((512,), F32)
        MT = AR.alloc((8, 512), BF16)
        TMP = AR.alloc((D_MODEL,), F32)
        XR = [AR.alloc((D_MODEL,), F32) for _ in range(1)]
        HO = [AR.alloc((D_MODEL,), F32) for _ in range(2)]
        for k in range(8):
            DMA('pool', 'wc', [], ['WinC'], out=WinC[:, k, 0:512], in_=w_in[k * 128:(k + 1) * 128, 0:512])
            DMA('pool', 'wc', [], ['WinC'], out=WinC[:, k, 512:2560], in_=w_in[k * 128:(k + 1) * 128, 2048:4096])
        for g in range(4):
            DMA('pool', 'wpm', [], ['Wpm'], out=Wpm[:, g, :], in_=pool_mix[g, :, :])
        load_w(Wa, w_a, 'wa', 'Wa', 4, D_MODEL)
        load_w(Wb, w_b, 'wb', 'Wb', 4, D_MODEL)
        load_w(Wo, w_out, 'wo', 'Wo', 8, D_MODEL)

        rot = [0]

        def nextbank():
            b = 2 + rot[0] % 4
            rot[0] += 1
            return b

        pair_ps = psall[:, 6:8, :].rearrange("p a b -> p (a b)")

        def post_norm_residual(res_tile, res_key, gi, dst_tile, dst_key, JUNK, TMP):
            c = stat_col()
            OP('act', 'activation', [('ps', 6), ('ps', 7)], ['junk', ('st', c)], out=JUNK, in_=pair_ps, func=AF.Square,
               accum_out=stat[:, c:c + 1])
            c2 = stat_col()
            OP('act', 'activation', [('st', c), 'epsc'], [('st', c2)], out=stat[:, c2:c2 + 1], in_=stat[:, c:c + 1],
               func=AF.Sqrt, bias=epsc[:], scale=1.0 / D_MODEL)
            c3 = stat_col()
            OP('dve', 'reciprocal', [('st', c2)], [('st', c3)], out=stat[:, c3:c3 + 1], in_=stat[:, c2:c2 + 1])
            OP('dve', 'scalar_tensor_tensor', [('ps', 6), ('ps', 7), ('st', c3), 'gpost'], ['TMP'], out=TMP, in0=pair_ps,
               scalar=stat[:, c3:c3 + 1], in1=gpost[:, gi, :], op0=ALU.mult, op1=ALU.mult)
            OP('pool', 'tensor_tensor', ['TMP', res_key], [dst_key], out=dst_tile, in0=TMP, in1=res_tile, op=ALU.add)

        for j in range(NOWN):
            for s in range(4):
                r0 = (2 * j * 4 + s) * 128
                load_norm_T(xfp[r0:r0 + 128, :], 128, 0, UTc[:, :, s * 128:(s + 1) * 128], 'utc', XT, UB, JUNK)
            load_norm_T(xh[j * 16:(j + 1) * 16, :], 16, 0, UTh[:, :, :], 'uth', XT, UB, JUNK)
            hb = nextbank()
            for g in range(4):
                for c in range(8):
                    OP('pe', 'matmul', ['WinC', 'uth'], [('ps', hb)], out=psbank(hb)[:, g * 16:(g + 1) * 16],
                       lhsT=WinC[:, c, g * 128:(g + 1) * 128], rhs=UTh[:, c, :], start=(c == 0), stop=(c == 7))
            OP('dve', 'tensor_copy', [('ps', hb)], ['PP'], out=PP[:, :, 0:16],
               in_=psbank(hb)[:, 0:64].rearrange("p (g t) -> p g t", g=4))
            for g in range(4):
                pb = nextbank()
                for c in range(8):
                    OP('pe', 'matmul', ['WinC', 'utc'], [('ps', pb)], out=psbank(pb), lhsT=WinC[:, c, g * 128:(g + 1) * 128],
                       rhs=UTc[:, c, :], start=(c == 0), stop=(c == 7))
                OP('act', 'activation', [('ps', pb)], ['PP'], out=PP[:, g, 16:528], in_=psbank(pb), func=AF.Copy)
            for g in range(4):
                w = 2 ** (g + 1)
                src = PP[:, g, :]
                src_key = 'PP'
                lo = 0
                k = 1
                tog = 0
                while k < w:
                    dst = TA if tog == 0 else TB
                    dkey = 'TA' if tog == 0 else 'TB'
                    nlo = lo + k
                    OP('pool', 'tensor_tensor', [src_key], [dkey], out=dst[:, nlo:528], in0=src[:, nlo:528],
                       in1=src[:, nlo - k:528 - k], op=ALU.add)
                    src, src_key, lo = dst, dkey, nlo
                    k *= 2
                    tog ^= 1
                OP('dve', 'scalar_tensor_tensor', [src_key, 'PP'], ['Y'], out=Y[:, g, :], in0=src[:, 16:528], scalar=1.0 / w,
                   in1=PP[:, g, 16:528], op0=ALU.mult, op1=ALU.subtract)
                if j == 0:
                    OP('pool', 'tensor_tensor', [src_key, 'icnt'], [src_key], out=src[:, 0:16], in0=src[:, 16:32],
                       in1=icnt[:, g, :], op=ALU.mult)
                    OP('pool', 'tensor_tensor', [src_key, 'PP'], ['Y'], out=Y[:, g, 0:16], in0=src[:, 0:16],
                       in1=PP[:, g, 16:32], op=ALU.subtract)
            def gate_pair(m):
                sl = m % 4
                for c in range(8):
                    OP('pe', 'matmul', ['WinC', 'utc'], [('ps', 2)], out=psbank(2),
                       lhsT=WinC[:, c, 512 + m * 128:512 + (m + 1) * 128], rhs=UTc[:, c, :], start=(c == 0), stop=(c == 7))
                OP('act', 'activation', [('ps', 2)], [('sa', sl)], out=SA4[:, sl, :], in_=psbank(2), func=AF.Sigmoid)
                for c in range(8):
                    OP('pe', 'matmul', ['WinC', 'utc'], [('ps', 3)], out=psbank(3),
                       lhsT=WinC[:, c, 1536 + m * 128:1536 + (m + 1) * 128], rhs=UTc[:, c, :], start=(c == 0), stop=(c == 7))
                OP('act', 'activation', [('ps', 3)], [('sb', sl)], out=SB4[:, sl, :], in_=psbank(3), func=AF.Sigmoid)

            def branch_pair(m):
                sl = m % 4
                for c in range(4):
                    OP('pe', 'matmul', ['Wa', 'YM'], [('ps', 4)], out=psbank(4), lhsT=Wa[:, c, m * 128:(m + 1) * 128],
                       rhs=YM[:, c, :], start=(c == 0), stop=(c == 3))
                OP('dve', 'tensor_tensor', [('ps', 4), ('sa', sl)], ['T1'], out=T1, in0=psbank(4), in1=SA4[:, sl, :], op=ALU.mult)
                for c in range(4):
                    OP('pe', 'matmul', ['Wb', 'OT'], [('ps', 5)], out=psbank(5), lhsT=Wb[:, c, m * 128:(m + 1) * 128],
                       rhs=OT[:, c, j * 512:(j + 1) * 512], start=(c == 0), stop=(c == 3))
                OP('dve', 'tensor_tensor', [('ps', 5), ('sb', sl)], ['T2'], out=T2, in0=psbank(5), in1=SB4[:, sl, :], op=ALU.mult)
                OP('pool', 'tensor_tensor', ['T1', 'T2'], ['MT'], out=MT[:, m, :], in0=T1, in1=T2, op=ALU.add)

            for m in range(4):
                gate_pair(m)
            for g in range(4):
                pb = 6 + (g % 2)
                OP('pe', 'matmul', ['Wpm', 'Y'], [('ps', pb)], out=psbank(pb), lhsT=Wpm[:, g, :], rhs=Y[:, g, :],
                   start=True, stop=True)
                OP('dve', 'tensor_scalar', [('ps', pb), 'gcols'], ['YM'], out=YM[:, g, :], in0=psbank(pb),
                   scalar1=gcols[:, 16 + g:17 + g], scalar2=None, op0=ALU.mult)
            for m in range(4):
                branch_pair(m)
            for m in range(4, 8):
                gate_pair(m)
                branch_pair(m)
            for s in range(4):
                r0 = (2 * j * 4 + s) * 128
                xi = 0
                DMA('sp', ('xr', xi), [], [('xr', xi)], out=XR[xi], in_=xfp[r0:r0 + 128, :])
                for n in range(2):
                    for c in range(8):
                        OP('pe', 'matmul', ['MT', 'Wo'], [('ps', 6 + n)], out=psbank(6 + n),
                           lhsT=MT[:, c, s * 128:(s + 1) * 128], rhs=Wo[:, c, n * 512:(n + 1) * 512],
                           start=(c == 0), stop=(c == 7))
                hi = (j * 4 + s) % 2
                post_norm_residual(XR[xi], ('xr', xi), 0, HO[hi], ('ho', hi), JUNK, TMP)
                o0 = (j * 4 + s) * 128
                DMA('sp', ('hos', hi), [('ho', hi)], [('h1s', j * 4 + s)], out=h1s[o0:o0 + 128, :], in_=HO[hi])

        checkpoint('C')
        barrier()
        AR.reset()
        Wg = AR.alloc((8, FFN), BF16)
        Wu = AR.alloc((8, FFN), BF16)
        Wd = AR.alloc((NFC, D_MODEL), BF16)
        XT = [AR.alloc((D_MODEL,), F32) for _ in range(4)]
        JUNK = AR.alloc((D_MODEL,), BF16)
        UB = [AR.alloc((D_MODEL,), BF16) for _ in range(2)]
        UTd = [AR.alloc((8, 256), BF16) for _ in range(2)]
        SG = [AR.alloc((256,), F32) for _ in range(2)]
        FT = AR.alloc((NFC, 256), BF16)
        TMP = AR.alloc((D_MODEL,), F32)
        HO = [AR.alloc((D_MODEL,), F32) for _ in range(1)]
        load_w(Wg, w_g, 'wg', 'Wg', 8, FFN)
        load_w(Wu, w_u, 'wu', 'Wu', 8, FFN)
        load_w(Wd, w_d, 'wd', 'Wd', NFC, D_MODEL)
        NT2 = SOWN // 256
        tile_bufs = {}

        def prepA_D(t):
            tile_bufs[t] = []
            for s in range(2):
                r0 = (t * 2 + s) * 128
                tile_bufs[t].append(prep_a(h1s[r0:r0 + 128, :], 128, XT, UB, JUNK, src_reads=[('h1s', t * 2 + s)]))

        def prepB_D(t):
            for s in range(2):
                prep_b(tile_bufs[t][s][1], 128, 8, UTd[t % 2][:, :, s * 128:(s + 1) * 128], ('utd', t % 2), UB)

        prepA_D(0)
        prepB_D(0)
        for t in range(NT2):
            utd = UTd[t % 2]
            if t + 1 < NT2:
                prepA_D(t + 1)
            for m in range(NFC):
                bg = 2 + (m % 2)
                bu = 4 + (m % 2)
                for c in range(8):
                    OP('pe', 'matmul', ['Wg', ('utd', t % 2)], [('ps', bg)], out=psbank(bg)[:, 0:256],
                       lhsT=Wg[:, c, m * 128:(m + 1) * 128], rhs=utd[:, c, :], start=(c == 0), stop=(c == 7))
                sg = SG[m % 2]
                OP('act', 'activation', [('ps', bg)], [('sg', m % 2)], out=sg, in_=psbank(bg)[:, 0:256], func=AF.Silu)
                for c in range(8):
                    OP('pe', 'matmul', ['Wu', ('utd', t % 2)], [('ps', bu)], out=psbank(bu)[:, 0:256],
                       lhsT=Wu[:, c, m * 128:(m + 1) * 128], rhs=utd[:, c, :], start=(c == 0), stop=(c == 7))
                OP('dve', 'tensor_tensor', [('ps', bu), ('sg', m % 2)], ['FT'], out=FT[:, m, :], in0=psbank(bu)[:, 0:256],
                   in1=sg, op=ALU.mult)
            for s in range(2):
                for n in range(2):
                    for m in range(NFC):
                        OP('pe', 'matmul', ['FT', 'Wd'], [('ps', 6 + n)], out=psbank(6 + n),
                           lhsT=FT[:, m, s * 128:(s + 1) * 128], rhs=Wd[:, m, n * 512:(n + 1) * 512],
                           start=(m == 0), stop=(m == NFC - 1))
                if s == 1 and t + 1 < NT2:
                    prepB_D(t + 1)
                hi = 0
                xb = tile_bufs[t][s][0]
                post_norm_residual(XT[xb], ('xt', xb), 1, HO[hi], ('ho', hi), JUNK, TMP)
                o0 = (t * 2 + s) * 128
                DMA('sp', ('hos', hi), [('ho', hi)], [('outd', t * 2 + s)], final=True, out=outd[o0:o0 + 128, :], in_=HO[hi])

        S.emit()
    return nc


_PROGRAM_CACHE = {}


def make_core_inputs(x, w_in, pool_mix, pool_scale, w_branch_a, lam_q1, lam_k1, lam_q2, lam_k2,
                     subln_g, w_branch_b, w_out, mix_pre_g, mix_post_g, ffn_pre_g, ffn_post_g,
                     w_ffn_gate, w_ffn_up, w_ffn_down):
    B, S_LEN, _ = x.shape
    NT = S_LEN // 512
    t0, t1 = own_tiles(NT)
    f = np.float32
    gcols = np.concatenate([
        np.asarray(mix_pre_g[0], f).reshape(8, 128).T,
        np.asarray(ffn_pre_g[0], f).reshape(8, 128).T,
        np.asarray(pool_scale[0], f).reshape(4, 128).T,
        np.asarray(subln_g[0], f).reshape(1, 128).T,
    ], axis=1)
    grows = np.stack([np.asarray(mix_post_g[0], f), np.asarray(ffn_post_g[0], f)], axis=0)
    lamv = np.stack([np.asarray(v[0], f) for v in (lam_q1, lam_k1, lam_q2, lam_k2)], axis=0)
    shared = {
        "w_in": np.ascontiguousarray(w_in[0], f), "pool_mix": np.ascontiguousarray(pool_mix[0], f),
        "w_a": np.ascontiguousarray(w_branch_a[0], f), "w_b": np.ascontiguousarray(w_branch_b[0], f),
        "w_out": np.ascontiguousarray(w_out[0], f), "w_g": np.ascontiguousarray(w_ffn_gate[0], f),
        "w_u": np.ascontiguousarray(w_ffn_up[0], f), "w_d": np.ascontiguousarray(w_ffn_down[0], f),
        "gcols": np.ascontiguousarray(gcols), "grows": np.ascontiguousarray(grows), "lamv": np.ascontiguousarray(lamv),
    }
    in_maps = []
    orders = []
    for core in range(2 * B):
        b, r = core // 2, core % 2
        own = t0 if r == 0 else t1
        oth = t1 if r == 0 else t0
        order = []
        for m in range(NT // 2):
            order += [own[m], oth[m]]
        xb = np.asarray(x[b], f).reshape(NT, 512, D_MODEL)
        xfp = np.ascontiguousarray(xb[order].reshape(S_LEN, D_MODEL))
        xhalo = np.zeros((NT // 2, 16, D_MODEL), f)
        for jj, t in enumerate(own):
            if t > 0:
                xhalo[jj] = x[b, t * 512 - 16:t * 512]
        ksv = np.tile(np.asarray([512.0 * t for t in order], f)[None, :], (128, 1))
        m = dict(shared)
        m.update({"xfp": xfp, "xh": np.ascontiguousarray(xhalo.reshape(-1, D_MODEL)), "ks": np.ascontiguousarray(ksv)})
        in_maps.append(m)
        orders.append(own)
    return in_maps, orders


def kernel(**inputs):
    inputs = {k: np.asarray(v) for k, v in inputs.items()}
    x = inputs["x"]
    B, S_LEN, _ = x.shape
    NT = S_LEN // 512
    if NT not in _PROGRAM_CACHE:
        _PROGRAM_CACHE[NT] = build_program(NT)
    nc = _PROGRAM_CACHE[NT]
    in_maps, orders = make_core_inputs(**inputs)
    res = run_bass_kernel_spmd(nc, in_maps, core_ids=list(range(2 * B)))
    out = np.empty((B, S_LEN, D_MODEL), np.float32)
    for core in range(2 * B):
        b = core // 2
        o = np.asarray(res.results[core]["out"]).reshape(NT // 2, 512, D_MODEL)
        for jj, t in enumerate(orders[core]):
            out[b, t * 512:(t + 1) * 512] = o[jj]
    return out
```

```python
import math
from contextlib import ExitStack

import numpy as np
import concourse.bass as bass
import concourse.mybir as mybir
from concourse.bass_utils import run_bass_kernel_spmd

F32 = mybir.dt.float32
BF16 = mybir.dt.bfloat16
AF = mybir.ActivationFunctionType
ALU = mybir.AluOpType

D_MODEL = 1024
FFN = 2816
NFC = FFN // 128
EPS = 1e-6
LAMBDA_INIT = 0.8 - 0.6 * math.exp(-0.3 * 0)
SLOPES = [0.25 ** (i + 1) for i in range(4)]
NEG_BIG = -30000.0
_STOP = [None]


class _Op:
    __slots__ = ('eng', 'fn', 'idx', 'is_dma', 'key', 'cum', 'sig', 'count', 'waits', 'clock')


class Sched:
    ENGS = ('pe', 'act', 'dve', 'pool', 'sp')

    def __init__(self, nc, es):
        self.nc = nc
        self.es = es
        self.ops = {e: [] for e in self.ENGS}
        self.last_writer = {}
        self.readers = {}
        self.dma_cum = {}
        self.seen = {e: {f: -1 for f in self.ENGS} for e in self.ENGS}
        self.seen_dma = {e: {} for e in self.ENGS}
        self.final_dmas = []

    def _add(self, eng, fn, reads, writes, is_dma, key, final=False):
        op = _Op()
        op.eng = eng
        op.fn = fn
        op.is_dma = is_dma
        op.key = key
        op.sig = False
        op.count = 0
        op.cum = 0
        reads = list(reads) + ['ARENA']
        raw = []
        other = []
        for r in reads:
            lw = self.last_writer.get(r)
            if lw is not None:
                raw.append(lw)
        for w in writes:
            lw = self.last_writer.get(w)
            if lw is not None:
                other.append(lw)
            rd = self.readers.get(w)
            if rd is not None:
                other.extend(rd[0].values())
                other.extend(rd[1].values())
        seen = self.seen[eng]
        sdma = self.seen_dma[eng]
        waits = []
        for kind, lst in ((0, raw), (1, other)):
            for d in lst:
                if d.is_dma:
                    if kind == 1 and is_dma and d.key == key:
                        continue
                    if sdma.get(d.key, 0) < d.cum:
                        waits.append(d)
                        sdma[d.key] = d.cum
                    continue
                f = d.eng
                if f == eng:
                    if eng == 'pe' or kind == 1:
                        continue
                if seen[f] < d.idx:
                    waits.append(d)
                    d.sig = True
                    ck = d.clock
                    for g in self.ENGS:
                        if ck[g] > seen[g]:
                            seen[g] = ck[g]
                    if seen[f] < d.idx:
                        seen[f] = d.idx
        op.waits = waits
        op.idx = len(self.ops[eng])
        self.ops[eng].append(op)
        if is_dma:
            cum = self.dma_cum.get(key, 0) + 16
            self.dma_cum[key] = cum
            op.cum = cum
            op.clock = None
            if final:
                self.final_dmas.append(op)
        else:
            ck = dict(seen)
            if ck[eng] < op.idx - 1:
                ck[eng] = op.idx - 1
            op.clock = ck
        for r in reads:
            rd = self.readers.get(r)
            if rd is None:
                rd = ({}, {})
                self.readers[r] = rd
            if is_dma:
                rd[1][key] = op
            else:
                rd[0][eng] = op
        for w in writes:
            self.last_writer[w] = op
            self.readers[w] = ({}, {})
        return op

    def op(self, eng, fn, reads=(), writes=()):
        return self._add(eng, fn, reads, writes, False, None)

    def dma(self, eng, fn, key, reads=(), writes=(), final=False):
        return self._add(eng, fn, reads, writes, True, key, final)

    def emit(self):
        nc = self.nc
        sems = {}
        for e in self.ENGS:
            sems[e] = self.es.enter_context(nc.semaphore("sem_" + e))
        dsems = {}
        for i, k in enumerate(self.dma_cum.keys()):
            dsems[k] = self.es.enter_context(nc.semaphore("dsem_%d" % i))
        for e in self.ENGS:
            c = 0
            for o in self.ops[e]:
                if o.sig and not o.is_dma:
                    c += 1
                    o.count = c
        fin_keys = {}
        for o in self.final_dmas:
            fin_keys[o.key] = max(fin_keys.get(o.key, 0), o.cum)

        def run(engname, eng):
            for o in self.ops[engname]:
                for d in o.waits:
                    if d.is_dma:
                        eng.wait_ge(dsems[d.key], d.cum)
                    else:
                        eng.wait_ge(sems[d.eng], d.count)
                name, kw = o.fn
                inst = getattr(eng, name)(**kw)
                if o.is_dma:
                    inst.then_inc(dsems[o.key], 16)
                elif o.sig:
                    inst.then_inc(sems[engname], 1)
            if engname == 'sp':
                for k, v in fin_keys.items():
                    eng.wait_ge(dsems[k], v)

        block = self.es.enter_context(nc.Block())
        block.sync(lambda e: run('sp', e))
        block.tensor(lambda e: run('pe', e))
        block.scalar(lambda e: run('act', e))
        block.vector(lambda e: run('dve', e))
        block.gpsimd(lambda e: run('pool', e))


def own_tiles(NT):
    t0, t1 = [], []
    for m in range(NT // 2):
        a, b = 2 * m, 2 * m + 1
        if m % 2 == 0:
            t0.append(a); t1.append(b)
        else:
            t0.append(b); t1.append(a)
    return t0, t1


def build_program(NT):
    S_LEN = NT * 512
    NOWN = NT // 2
    SOWN = NOWN * 512
    NKT = NT * 4
    nc = bass.Bass("TRN2", target_bir_lowering=False)

    def din(name, shape):
        return nc.dram_tensor(name, shape, F32, kind="ExternalInput").ap()

    xfp = din("xfp", [S_LEN, D_MODEL])
    xh = din("xh", [NOWN * 16, D_MODEL])
    ksd = din("ks", [128, NT])
    w_in = din("w_in", [D_MODEL, 4096])
    pool_mix = din("pool_mix", [4, 128, 128])
    w_a = din("w_a", [512, D_MODEL])
    w_b = din("w_b", [512, D_MODEL])
    w_out = din("w_out", [D_MODEL, D_MODEL])
    w_g = din("w_g", [D_MODEL, FFN])
    w_u = din("w_u", [D_MODEL, FFN])
    w_d = din("w_d", [FFN, D_MODEL])
    gcolsd = din("gcols", [128, 21])
    growsd = din("grows", [2, D_MODEL])
    lamd = din("lamv", [4, 64])
    outd = nc.dram_tensor("out", [SOWN, D_MODEL], F32, kind="ExternalOutput").ap()
    h1s = nc.dram_tensor("h1s", [SOWN, D_MODEL], F32, kind="ExternalOutput").ap()

    es = ExitStack()
    with es:
        def sb(name, shape, dt):
            return es.enter_context(nc.sbuf_tensor(name, shape, dt))

        S = Sched(nc, es)

        stopped = [False]

        def OP(eng, name, reads, writes, **kw):
            if not stopped[0]:
                S.op(eng, (name, kw), reads, writes)

        def DMA(eng, key, reads, writes, final=False, **kw):
            if not stopped[0]:
                S.dma(eng, ('dma_start', kw), key, reads, writes, final)

        def checkpoint(name):
            if _STOP[0] == name:
                stopped[0] = True

        ident = sb("ident", [128, 128], BF16)
        identf = sb("identf", [128, 128], F32)
        ones = sb("ones", [128, 128], BF16)
        onesf = sb("onesf", [128, 128], F32)
        gcols = sb("gcols_s", [128, 21], F32)
        gpost = sb("gpost", [128, 2, D_MODEL], F32)
        lamb = sb("lamb", [128, 4, 64], F32)
        lamt = sb("lamt", [128, 2, 64], F32)
        lams = sb("lams", [128, 8], F32)
        sublg = sb("sublg", [128, 1], F32)
        Dqp = sb("Dqp", [128, 512], F32)
        iotaq = sb("iotaq", [128, 512], F32)
        iota4 = sb("iota4", [128, 4], F32)
        iota4q = sb("iota4q", [128, 4], F32)
        masks = sb("masks", [128, 4, 512], BF16)
        ks = sb("ks_s", [128, NT], F32)
        kpos = sb("kpos", [128, NT, 4], F32)
        pen = sb("pen", [128, NOWN], F32)
        qref = sb("qref", [128, NOWN, 4], F32)
        icnt = sb("icnt", [128, 4, 16], F32)
        stat = sb("stat", [128, 64], F32)
        bar = sb("bar", [128, 8], F32)
        epsc = sb("epsc", [128, 1], F32)
        ARENA_BYTES = 184 * 1024
        arena_t = sb("arena", [128, ARENA_BYTES // 2], BF16)
        psall = es.enter_context(nc.psum_tensor("psall", [128, 8, 512], F32))

        class Arena:
            def __init__(self):
                self.off = 0

            def reset(self):
                self.off = 0

            def alloc(self, free_shape, dt, parts=128):
                n = 1
                for v in free_shape:
                    n *= v
                esz = 4 if dt == F32 else 2
                nb = n * esz
                nb_al = (nb + 63) // 64 * 64
                assert self.off + nb_al <= ARENA_BYTES, ("arena overflow", self.off, nb_al)
                a = arena_t[0:parts, self.off // 2:(self.off + nb) // 2]
                self.off += nb_al
                if dt == F32:
                    a = a.bitcast(F32)
                if len(free_shape) == 2:
                    a = a.rearrange("p (a b) -> p a b", a=free_shape[0])
                elif len(free_shape) == 3:
                    a = a.rearrange("p (a b c) -> p a b c", a=free_shape[0], b=free_shape[1])
                return a

        AR = Arena()

        def psbank(b):
            return psall[:, b, :]

        def psbank_bf(b):
            return psall[:, b, :].bitcast(BF16)

        stat_ctr = [0]

        def stat_col():
            c = stat_ctr[0] % 64
            stat_ctr[0] += 1
            return c

        def barrier():
            OP('dve', 'memset', [], ['ARENA'], ap=bar[:, 0:1], constant=0.0)

        AX = mybir.AxisListType.X

        DMA('sp', 'c0', [], ['gcols'], out=gcols[:], in_=gcolsd)
        DMA('sp', 'c1', [], ['ks'], out=ks[:], in_=ksd)
        for i in range(2):
            DMA('sp', 'c2', [], ['gpost'], out=gpost[:, i, :], in_=growsd[i:i + 1, :].partition_broadcast(128))
        for i in range(4):
            DMA('sp', 'c3', [], ['lamb'], out=lamb[:, i, :], in_=lamd[i:i + 1, :].partition_broadcast(128))
        OP('dve', 'memset', [], ['epsc'], ap=epsc[:], constant=EPS)
        OP('pool', 'memset', [], ['identf'], ap=identf[:], constant=0.0)
        OP('pool', 'affine_select', ['identf'], ['identf'], out=identf[:], in_=identf[:], pattern=[[-1, 128]],
           compare_op=ALU.not_equal, fill=1.0, base=0, channel_multiplier=1)
        OP('dve', 'tensor_copy', ['identf'], ['ident'], out=ident[:], in_=identf[:])
        OP('dve', 'memset', [], ['ones'], ap=ones[:], constant=1.0)
        OP('dve', 'memset', [], ['onesf'], ap=onesf[:], constant=1.0)
        OP('pool', 'iota', [], ['Dqp'], out=Dqp[:], pattern=[[1, 512]], base=0, channel_multiplier=-1,
           allow_small_or_imprecise_dtypes=True)
        OP('pool', 'iota', [], ['iotaq'], out=iotaq[:], pattern=[[1, 512]], base=0, channel_multiplier=0,
           allow_small_or_imprecise_dtypes=True)
        OP('pool', 'iota', [], ['iota4'], out=iota4[:], pattern=[[128, 4]], base=0, channel_multiplier=1,
           allow_small_or_imprecise_dtypes=True)
        OP('pool', 'iota', [], ['iota4q'], out=iota4q[:], pattern=[[128, 4]], base=127, channel_multiplier=0,
           allow_small_or_imprecise_dtypes=True)
        for s in range(4):
            OP('dve', 'tensor_single_scalar', ['Dqp'], ['masks'], out=masks[:, s, :], in_=Dqp[:], scalar=float(128 * s),
               op=ALU.is_ge)
        for i in range(NT):
            OP('dve', 'tensor_scalar', ['iota4', 'ks'], ['kpos'], out=kpos[:, i, :], in0=iota4[:], scalar1=ks[:, i:i + 1],
               scalar2=None, op0=ALU.add)
        for j in range(NOWN):
            OP('dve', 'tensor_tensor', ['ks'], ['pen'], out=pen[:, j:j + 1], in0=ks[:, 2 * j + 1:2 * j + 2],
               in1=ks[:, 2 * j:2 * j + 1], op=ALU.is_gt)
            OP('dve', 'tensor_scalar', ['ks', 'iota4q'], ['qref'], out=qref[:, j, :], in0=iota4q[:],
               scalar1=ks[:, 2 * j:2 * j + 1], scalar2=None, op0=ALU.add)
        OP('dve', 'tensor_scalar', ['pen'], ['pen'], out=pen[:], in0=pen[:], scalar1=NEG_BIG, scalar2=None, op0=ALU.mult)
        OP('dve', 'tensor_tensor', ['lamb'], ['lamt'], out=lamt[:, 0, :], in0=lamb[:, 0, :], in1=lamb[:, 1, :], op=ALU.mult)
        OP('dve', 'tensor_tensor', ['lamb'], ['lamt'], out=lamt[:, 1, :], in0=lamb[:, 2, :], in1=lamb[:, 3, :], op=ALU.mult)
        OP('dve', 'reduce_sum', ['lamt'], ['lams'], out=lams[:, 0:2], in_=lamt[:], axis=AX)
        OP('act', 'activation', ['lams'], ['lams2'], out=lams[:, 2:4], in_=lams[:, 0:2], func=AF.Exp)
        OP('dve', 'scalar_tensor_tensor', ['lams2'], ['neglam'], out=lams[:, 4:5], in0=lams[:, 3:4], scalar=-LAMBDA_INIT,
           in1=lams[:, 2:3], op0=ALU.add, op1=ALU.subtract)
        OP('dve', 'tensor_scalar', ['gcols'], ['sublg'], out=sublg[:], in0=gcols[:, 20:21], scalar1=1.0 - LAMBDA_INIT,
           scalar2=None, op0=ALU.mult)
        for g in range(4):
            OP('dve', 'tensor_scalar', ['iotaq', 'ks'], ['icnt'], out=icnt[:, g, :], in0=iotaq[:, 1:17], scalar1=ks[:, 0:1],
               scalar2=float(2 ** (g + 1)), op0=ALU.add, op1=ALU.min)
        OP('dve', 'reciprocal', ['icnt'], ['icnt'], out=icnt[:], in_=icnt[:])

        ctr = {'xt': 0, 'ub': 0, 'tp': 0}

        def prep_a(src_ap, nparts, XT, UB, JUNK, src_reads=()):
            b = ctr['xt'] % len(XT)
            ctr['xt'] += 1
            xt = XT[b]
            DMA('sp', ('xt', b), list(src_reads), [('xt', b)], out=xt[0:nparts, :], in_=src_ap)
            c = stat_col()
            OP('act', 'activation', [('xt', b)], ['junk', ('st', c)], out=JUNK[0:nparts, :], in_=xt[0:nparts, :],
               func=AF.Square, accum_out=stat[0:nparts, c:c + 1])
            c2 = stat_col()
            OP('act', 'activation', [('st', c), 'epsc'], [('st', c2)], out=stat[0:nparts, c2:c2 + 1],
               in_=stat[0:nparts, c:c + 1], func=AF.Sqrt, bias=epsc[0:nparts, :], scale=1.0 / D_MODEL)
            c3 = stat_col()
            OP('dve', 'reciprocal', [('st', c2)], [('st', c3)], out=stat[0:nparts, c3:c3 + 1], in_=stat[0:nparts, c2:c2 + 1])
            ub_i = ctr['ub'] % len(UB)
            ctr['ub'] += 1
            ub = UB[ub_i]
            OP('dve', 'tensor_scalar', [('xt', b), ('st', c3)], [('ub', ub_i)], out=ub[0:nparts, :], in0=xt[0:nparts, :],
               scalar1=stat[0:nparts, c3:c3 + 1], scalar2=None, op0=ALU.mult)
            return b, ub_i

        def prep_b(ub_i, nparts, gco, ut_dst, ut_key, UB):
            ub = UB[ub_i]
            tb = ctr['tp'] % 2
            ctr['tp'] += 1
            pst = psbank_bf(tb).rearrange("p (c n) -> p c n", c=8)
            for cc in range(8):
                OP('pe', 'transpose', [('ub', ub_i), 'ident'], [('ps', tb)], out=pst[:, cc, 0:nparts],
                   in_=ub[0:nparts, cc * 128:(cc + 1) * 128], identity=ident[0:nparts, 0:nparts])
            gb = gcols[:, gco:gco + 8].unsqueeze(2).to_broadcast([128, 8, nparts])
            OP('dve', 'tensor_tensor', [('ps', tb), 'gcols'], [ut_key], out=ut_dst, in0=pst[:, :, 0:nparts], in1=gb, op=ALU.mult)

        def load_norm_T(src_ap, nparts, gco, ut_dst, ut_key, XT, UB, JUNK, src_reads=()):
            b, ub_i = prep_a(src_ap, nparts, XT, UB, JUNK, src_reads)
            prep_b(ub_i, nparts, gco, ut_dst, ut_key, UB)
            return b

        def load_w(dst, src, key, wkey, kchunks, ncols):
            step = 2048
            for k in range(kchunks):
                for c0 in range(0, ncols, step):
                    c1 = min(ncols, c0 + step)
                    DMA('pool', key, [], [wkey], out=dst[:, k, c0:c1], in_=src[k * 128:(k + 1) * 128, c0:c1])

        checkpoint('SETUP')
        for hp in range(2):
            barrier()
            AR.reset()
            OT = AR.alloc((4, SOWN), BF16)
            KT = AR.alloc((2, S_LEN), BF16)
            V = AR.alloc((NKT, 256), BF16)
            QT = AR.alloc((2, SOWN), BF16)
            Wq = AR.alloc((8, 256), BF16)
            Wk = AR.alloc((8, 256), BF16)
            Wv = AR.alloc((8, 256), BF16)
            XT = [AR.alloc((D_MODEL,), F32) for _ in range(3)]
            JUNK = AR.alloc((D_MODEL,), BF16)
            UB = [AR.alloc((D_MODEL,), BF16) for _ in range(4)]
            UT = [AR.alloc((8, 512), BF16) for _ in range(2)]
            PT = [AR.alloc((512,), BF16) for _ in range(4)]
            BIAS = [AR.alloc((4, NKT), F32) for _ in range(2)]
            R = [AR.alloc((512,), F32) for _ in range(1)]
            A = [AR.alloc((512,), F32) for _ in range(2)]
            SQ = AR.alloc((512,), BF16)
            RT = AR.alloc((512,), F32)
            AO = AR.alloc((512,), F32)
            QBD = [AR.alloc((2, 512), BF16) for _ in range(2)]
            c_q = 512 + hp * 256
            c_k = 1024 + hp * 256
            c_v = 1536 + hp * 256
            load_w(Wk, w_in[:, c_k:c_k + 256], 'wk', 'Wk', 8, 256)
            load_w(Wv, w_in[:, c_v:c_v + 256], 'wv', 'Wv', 8, 256)
            load_w(Wq, w_in[:, c_q:c_q + 256], 'wq', 'Wq', 8, 256)

            slot_ub = {}

            def prepA(i):
                slot_ub[i] = []
                for s in range(4):
                    r0 = (i * 4 + s) * 128
                    slot_ub[i].append(prep_a(xfp[r0:r0 + 128, :], 128, XT, UB, JUNK)[1])

            def prepB(i):
                for s in range(4):
                    prep_b(slot_ub[i][s], 128, 0, UT[i % 2][:, :, s * 128:(s + 1) * 128], ('ut', i % 2), UB)

            prepA(0)
            prepB(0)
            for i in range(NT):
                ub_ = i % 2
                ut = UT[ub_]
                if i + 1 < NT:
                    prepA(i + 1)
                for hl in range(2):
                    pb = 2 + hl
                    for c in range(8):
                        OP('pe', 'matmul', [('ut', ub_), 'Wk'], [('ps', pb)], out=psbank(pb),
                           lhsT=Wk[:, c, hl * 128:(hl + 1) * 128], rhs=ut[:, c, :], start=(c == 0), stop=(c == 7))
                    OP('act', 'activation', [('ps', pb)], ['KT'], out=KT[:, hl, i * 512:(i + 1) * 512], in_=psbank(pb),
                       func=AF.Copy)
                for s in range(4):
                    pb = 4 + (s % 2)
                    for c in range(8):
                        OP('pe', 'matmul', [('ut', ub_), 'Wv'], [('ps', pb)], out=psbank(pb)[:, 0:256],
                           lhsT=ut[:, c, s * 128:(s + 1) * 128], rhs=Wv[:, c, :], start=(c == 0), stop=(c == 7))
                    OP('dve', 'tensor_copy', [('ps', pb)], ['V'], out=V[:, i * 4 + s, :], in_=psbank(pb)[:, 0:256])
                if i % 2 == 0:
                    j = i // 2
                    for hl in range(2):
                        pb = 6 + hl
                        for c in range(8):
                            OP('pe', 'matmul', [('ut', ub_), 'Wq'], [('ps', pb)], out=psbank(pb),
                               lhsT=Wq[:, c, hl * 128:(hl + 1) * 128], rhs=ut[:, c, :], start=(c == 0), stop=(c == 7))
                        OP('act', 'activation', [('ps', pb)], ['QT'], out=QT[:, hl, j * 512:(j + 1) * 512], in_=psbank(pb),
                           func=AF.Copy)
                if i + 1 < NT:
                    prepB(i + 1)

            checkpoint('A%d' % hp)
            kpf = kpos[:].rearrange("p a b -> p (a b)")
            steps = []
            for hl in range(2):
                for j in range(NOWN):
                    nkt = 4 * (2 * j + 2)
                    lst = []
                    for kt in range(nkt):
                        for qb in range(2):
                            if kt // 4 == 2 * j and qb == 0 and kt % 4 >= 2:
                                continue
                            lst.append((kt, qb))
                    firsts = {}
                    lasts = {}
                    for idx, (kt, qb) in enumerate(lst):
                        firsts.setdefault(qb, idx)
                        lasts[qb] = idx
                    for idx, (kt, qb) in enumerate(lst):
                        steps.append((hl, j, kt, qb, nkt, idx == firsts[qb], idx == lasts[qb], idx == len(lst) - 1))
            NSTEP = len(steps)
            LA = 2
            for b2 in range(2):
                OP('pool', 'memset', [], [('qbd', b2)], ap=QBD[b2], constant=0.0)

            def emit_front(n):
                hl, j, kt, qb, nkt, first, last, glast = steps[n]
                h = 2 * hp + hl
                slope = SLOPES[h]
                gi = hl * NOWN + j
                bi = gi % 2
                bias = BIAS[bi]
                qbd = QBD[bi]
                if kt == 0 and qb == 0:
                    refs = range(4) if h == 0 else (1, 3)
                    for r in refs:
                        OP('dve', 'tensor_scalar', ['kpos', 'qref'], [('bias', bi)], out=bias[:, r, 0:nkt], in0=kpf[:, 0:nkt],
                           scalar1=qref[:, j, r:r + 1], scalar2=0.0, op0=ALU.subtract, op1=ALU.min)
                        OP('dve', 'tensor_scalar', [('bias', bi)], [('bias', bi)], out=bias[:, r, 0:nkt],
                           in0=bias[:, r, 0:nkt], scalar1=slope, scalar2=None, op0=ALU.mult)
                        OP('dve', 'tensor_scalar', [('bias', bi), 'pen'], [('bias', bi)], out=bias[:, r, nkt - 4:nkt],
                           in0=bias[:, r, nkt - 4:nkt], scalar1=pen[:, j:j + 1], scalar2=None, op0=ALU.add)
                    for q2 in range(2):
                        c0 = j * 512 + q2 * 256
                        OP('pool', 'tensor_copy', ['QT'], [('qbd', bi)], out=qbd[0:64, q2, 0:256], in_=QT[0:64, hl, c0:c0 + 256])
                        OP('pool', 'tensor_copy', ['QT'], [('qbd', bi)], out=qbd[64:128, q2, 256:512],
                           in_=QT[64:128, hl, c0:c0 + 256])
                sbank = n % 3
                pi = n % 4
                pt = PT[pi]
                OP('pe', 'matmul', ['KT', ('qbd', bi)], [('ps', sbank)], out=psbank(sbank),
                   lhsT=KT[:, hl, kt * 128:(kt + 1) * 128], rhs=qbd[:, qb, :], start=True, stop=True)
                if h != 0:
                    r = 2 * qb + 1
                    OP('act', 'activation', [('ps', sbank), ('bias', bi)], [('pt', pi)], out=pt, in_=psbank(sbank),
                       func=AF.Exp, bias=bias[:, r, kt:kt + 1], scale=0.125)
                else:
                    ptv = pt.rearrange("p (m s q) -> p m s q", m=2, s=2)
                    psv = psbank(sbank).rearrange("p (m s q) -> p m s q", m=2, s=2)
                    for sbk in range(2):
                        r = 2 * qb + sbk
                        OP('act', 'activation', [('ps', sbank), ('bias', bi)], [('pt', pi)], out=ptv[:, :, sbk, :],
                           in_=psv[:, :, sbk, :], func=AF.Exp, bias=bias[:, r, kt:kt + 1], scale=0.125)
                s4 = kt % 4
                if kt // 4 == 2 * j and ((qb == 0 and s4 < 2) or (qb == 1 and s4 >= 2)):
                    ptm = pt.rearrange("p (m q) -> p m q", m=2)
                    mk = masks[:, s4, qb * 256:(qb + 1) * 256].unsqueeze(1).to_broadcast([128, 2, 256])
                    OP('dve', 'tensor_tensor', [('pt', pi), 'masks'], [('pt', pi)], out=ptm, in0=ptm, in1=mk, op=ALU.mult)

            def emit_back(n):
                hl, j, kt, qb, nkt, first, last, glast = steps[n]
                h = 2 * hp + hl
                pi = n % 4
                pt = PT[pi]
                ob = 3 + 2 * qb
                lb = 4 + 2 * qb
                OP('pe', 'matmul', ['V', ('pt', pi)], [('ps', ob)], out=psbank(ob),
                   lhsT=V[:, kt, hl * 128:(hl + 1) * 128], rhs=pt, start=first, stop=last)
                OP('pe', 'matmul', ['ones', ('pt', pi)], [('ps', lb)], out=psbank(lb), lhsT=ones[:], rhs=pt,
                   start=first, stop=last)
                if not glast:
                    return
                for q2 in range(2):
                    OP('dve', 'reciprocal', [('ps', 4 + 2 * q2)], [('R', 0)], out=R[0], in_=psbank(4 + 2 * q2))
                    OP('dve', 'tensor_tensor', [('ps', 3 + 2 * q2), ('R', 0)], [('A', q2)], out=A[q2],
                       in0=psbank(3 + 2 * q2), in1=R[0], op=ALU.mult)
                    OP('dve', 'scalar_tensor_tensor', [('A', q2), 'neglam'], ['AO'], out=AO[:, q2 * 256:(q2 + 1) * 256],
                       in0=A[q2][:, 256:512], scalar=lams[:, 4:5], in1=A[q2][:, 0:256], op0=ALU.mult, op1=ALU.add)
                OP('act', 'activation', ['AO'], ['SQ'], out=SQ, in_=AO, func=AF.Square)
                OP('pe', 'matmul', ['ones', 'SQ'], [('ps', 7)], out=psbank(7), lhsT=ones[:], rhs=SQ, start=True, stop=True)
                OP('act', 'activation', [('ps', 7), 'epsc'], ['RT'], out=RT, in_=psbank(7), func=AF.Sqrt, bias=epsc[:],
                   scale=1.0 / 128)
                OP('dve', 'reciprocal', ['RT'], [('R', 0)], out=R[0], in_=RT)
                OP('dve', 'scalar_tensor_tensor', ['AO', ('R', 0), 'sublg'], ['OT'],
                   out=OT[:, h, j * 512:(j + 1) * 512], in0=AO, scalar=sublg[:, 0:1], in1=R[0],
                   op0=ALU.mult, op1=ALU.mult)

            for n in range(NSTEP + LA):
                if n < NSTEP:
                    emit_front(n)
                if n >= LA:
                    emit_back(n - LA)

        checkpoint('B')
        barrier()
        AR.reset()
        OT = AR.alloc((4, SOWN), BF16)
        WinC = AR.alloc((8, 2560), BF16)
        Wpm = AR.alloc((4, 128), BF16)
        Wa = AR.alloc((4, D_MODEL), BF16)
        Wb = AR.alloc((4, D_MODEL), BF16)
        Wo = AR.alloc((8, D_MODEL), BF16)
        XT = [AR.alloc((D_MODEL,), F32) for _ in range(2)]
        JUNK = AR.alloc((D_MODEL,), BF16)
        UB = [AR.alloc((D_MODEL,), BF16) for _ in range(5)]
        UTc = AR.alloc((8, 512), BF16)
        UTh = AR.alloc((8, 16), BF16)
        PP = AR.alloc((4, 528), F32)
        TA = AR.alloc((528,), F32)
        TB = AR.alloc((528,), F32)
        Y = AR.alloc((4, 512), BF16)
        YM = AR.alloc((4, 512), BF16)
        SA4 = AR.alloc((4, 512), BF16)
        SB4 = AR.alloc((4, 512), BF16)
        T1 = AR.alloc((512,), BF16)
        T2 = AR.alloc((512,), BF16)
        MT = AR.alloc((8, 512), BF16)
        TMP = AR.alloc((D_MODEL,), F32)
        XR = [AR.alloc((D_MODEL,), F32) for _ in range(1)]
        HO = [AR.alloc((D_MODEL,), F32) for _ in range(1)]
        for k in range(8):
            DMA('pool', 'wc', [], ['WinC'], out=WinC[:, k, 0:512], in_=w_in[k * 128:(k + 1) * 128, 0:512])
            DMA('pool', 'wc', [], ['WinC'], out=WinC[:, k, 512:2560], in_=w_in[k * 128:(k + 1) * 128, 2048:4096])
        for g in range(4):
            DMA('pool', 'wpm', [], ['Wpm'], out=Wpm[:, g, :], in_=pool_mix[g, :, :])
        load_w(Wa, w_a, 'wa', 'Wa', 4, D_MODEL)
        load_w(Wb, w_b, 'wb', 'Wb', 4, D_MODEL)
        load_w(Wo, w_out, 'wo', 'Wo', 8, D_MODEL)

        rot = [0]

        def nextbank():
            b = 2 + rot[0] % 4
            rot[0] += 1
            return b

        def pair_view(b0):
            return psall[:, b0:b0 + 2, :].rearrange("p a b -> p (a b)")

        def post_norm_residual(b0, res_tile, res_key, gi, dst_tile, dst_key, JUNK, TMP):
            pair_ps = pair_view(b0)
            pk = [('ps', b0), ('ps', b0 + 1)]
            c = stat_col()
            OP('act', 'activation', pk, ['junk', ('st', c)], out=JUNK, in_=pair_ps, func=AF.Square,
               accum_out=stat[:, c:c + 1])
            c2 = stat_col()
            OP('act', 'activation', [('st', c), 'epsc'], [('st', c2)], out=stat[:, c2:c2 + 1], in_=stat[:, c:c + 1],
               func=AF.Sqrt, bias=epsc[:], scale=1.0 / D_MODEL)
            c3 = stat_col()
            OP('dve', 'reciprocal', [('st', c2)], [('st', c3)], out=stat[:, c3:c3 + 1], in_=stat[:, c2:c2 + 1])
            OP('dve', 'scalar_tensor_tensor', pk + [('st', c3), 'gpost'], ['TMP'], out=TMP, in0=pair_ps,
               scalar=stat[:, c3:c3 + 1], in1=gpost[:, gi, :], op0=ALU.mult, op1=ALU.mult)
            OP('pool', 'tensor_tensor', ['TMP', res_key], [dst_key], out=dst_tile, in0=TMP, in1=res_tile, op=ALU.add)

        c_ub = {}

        def prepA_C(j):
            c_ub[j] = []
            for s in range(4):
                r0 = (2 * j * 4 + s) * 128
                c_ub[j].append(prep_a(xfp[r0:r0 + 128, :], 128, XT, UB, JUNK)[1])
            c_ub[j].append(prep_a(xh[j * 16:(j + 1) * 16, :], 16, XT, UB, JUNK)[1])

        def prepB_C(j):
            for s in range(4):
                prep_b(c_ub[j][s], 128, 0, UTc[:, :, s * 128:(s + 1) * 128], 'utc', UB)
            prep_b(c_ub[j][4], 16, 0, UTh[:, :, :], 'uth', UB)

        prepA_C(0)
        prepB_C(0)
        for j in range(NOWN):
            hb = nextbank()
            for g in range(4):
                for c in range(8):
                    OP('pe', 'matmul', ['WinC', 'uth'], [('ps', hb)], out=psbank(hb)[:, g * 16:(g + 1) * 16],
                       lhsT=WinC[:, c, g * 128:(g + 1) * 128], rhs=UTh[:, c, :], start=(c == 0), stop=(c == 7))
            OP('dve', 'tensor_copy', [('ps', hb)], ['PP'], out=PP[:, :, 0:16],
               in_=psbank(hb)[:, 0:64].rearrange("p (g t) -> p g t", g=4))
            for g in range(4):
                pb = nextbank()
                for c in range(8):
                    OP('pe', 'matmul', ['WinC', 'utc'], [('ps', pb)], out=psbank(pb), lhsT=WinC[:, c, g * 128:(g + 1) * 128],
                       rhs=UTc[:, c, :], start=(c == 0), stop=(c == 7))
                OP('act', 'activation', [('ps', pb)], ['PP'], out=PP[:, g, 16:528], in_=psbank(pb), func=AF.Copy)
            for g in range(4):
                w = 2 ** (g + 1)
                src = PP[:, g, :]
                src_key = 'PP'
                lo = 0
                k = 1
                tog = 0
                while k < w:
                    dst = TA if tog == 0 else TB
                    dkey = 'TA' if tog == 0 else 'TB'
                    nlo = lo + k
                    OP('pool', 'tensor_tensor', [src_key], [dkey], out=dst[:, nlo:528], in0=src[:, nlo:528],
                       in1=src[:, nlo - k:528 - k], op=ALU.add)
                    src, src_key, lo = dst, dkey, nlo
                    k *= 2
                    tog ^= 1
                OP('dve', 'scalar_tensor_tensor', [src_key, 'PP'], ['Y'], out=Y[:, g, :], in0=src[:, 16:528], scalar=1.0 / w,
                   in1=PP[:, g, 16:528], op0=ALU.mult, op1=ALU.subtract)
                if j == 0:
                    OP('pool', 'tensor_tensor', [src_key, 'icnt'], [src_key], out=src[:, 0:16], in0=src[:, 16:32],
                       in1=icnt[:, g, :], op=ALU.mult)
                    OP('pool', 'tensor_tensor', [src_key, 'PP'], ['Y'], out=Y[:, g, 0:16], in0=src[:, 0:16],
                       in1=PP[:, g, 16:32], op=ALU.subtract)
            def gate_pair(m):
                sl = m % 4
                for c in range(8):
                    OP('pe', 'matmul', ['WinC', 'utc'], [('ps', 2)], out=psbank(2),
                       lhsT=WinC[:, c, 512 + m * 128:512 + (m + 1) * 128], rhs=UTc[:, c, :], start=(c == 0), stop=(c == 7))
                OP('act', 'activation', [('ps', 2)], [('sa', sl)], out=SA4[:, sl, :], in_=psbank(2), func=AF.Sigmoid)
                for c in range(8):
                    OP('pe', 'matmul', ['WinC', 'utc'], [('ps', 3)], out=psbank(3),
                       lhsT=WinC[:, c, 1536 + m * 128:1536 + (m + 1) * 128], rhs=UTc[:, c, :], start=(c == 0), stop=(c == 7))
                OP('act', 'activation', [('ps', 3)], [('sb', sl)], out=SB4[:, sl, :], in_=psbank(3), func=AF.Sigmoid)

            def branch_pair(m):
                sl = m % 4
                for c in range(4):
                    OP('pe', 'matmul', ['Wa', 'YM'], [('ps', 4)], out=psbank(4), lhsT=Wa[:, c, m * 128:(m + 1) * 128],
                       rhs=YM[:, c, :], start=(c == 0), stop=(c == 3))
                OP('dve', 'tensor_tensor', [('ps', 4), ('sa', sl)], ['T1'], out=T1, in0=psbank(4), in1=SA4[:, sl, :], op=ALU.mult)
                for c in range(4):
                    OP('pe', 'matmul', ['Wb', 'OT'], [('ps', 5)], out=psbank(5), lhsT=Wb[:, c, m * 128:(m + 1) * 128],
                       rhs=OT[:, c, j * 512:(j + 1) * 512], start=(c == 0), stop=(c == 3))
                OP('dve', 'tensor_tensor', [('ps', 5), ('sb', sl)], ['T2'], out=T2, in0=psbank(5), in1=SB4[:, sl, :], op=ALU.mult)
                OP('pool', 'tensor_tensor', ['T1', 'T2'], ['MT'], out=MT[:, m, :], in0=T1, in1=T2, op=ALU.add)

            if j + 1 < NOWN:
                prepA_C(j + 1)
            for m in range(4):
                gate_pair(m)
            for g in range(4):
                pb = 6 + (g % 2)
                OP('pe', 'matmul', ['Wpm', 'Y'], [('ps', pb)], out=psbank(pb), lhsT=Wpm[:, g, :], rhs=Y[:, g, :],
                   start=True, stop=True)
                OP('dve', 'tensor_scalar', [('ps', pb), 'gcols'], ['YM'], out=YM[:, g, :], in0=psbank(pb),
                   scalar1=gcols[:, 16 + g:17 + g], scalar2=None, op0=ALU.mult)
            for m in range(4):
                branch_pair(m)
            for m in range(4, 8):
                gate_pair(m)
                branch_pair(m)
            if j + 1 < NOWN:
                prepB_C(j + 1)
            for s in range(4):
                r0 = (2 * j * 4 + s) * 128
                xi = 0
                DMA('sp', ('xr', xi), [], [('xr', xi)], out=XR[xi], in_=xfp[r0:r0 + 128, :])
                b0 = 6 if s % 2 == 0 else 4
                for n in range(2):
                    for c in range(8):
                        OP('pe', 'matmul', ['MT', 'Wo'], [('ps', b0 + n)], out=psbank(b0 + n),
                           lhsT=MT[:, c, s * 128:(s + 1) * 128], rhs=Wo[:, c, n * 512:(n + 1) * 512],
                           start=(c == 0), stop=(c == 7))
                hi = 0
                post_norm_residual(b0, XR[xi], ('xr', xi), 0, HO[hi], ('ho', hi), JUNK, TMP)
                o0 = (j * 4 + s) * 128
                DMA('sp', ('hos', hi), [('ho', hi)], [('h1s', j * 4 + s)], out=h1s[o0:o0 + 128, :], in_=HO[hi])

        checkpoint('C')
        barrier()
        AR.reset()
        Wg = AR.alloc((8, FFN), BF16)
        Wu = AR.alloc((8, FFN), BF16)
        Wd = AR.alloc((NFC, D_MODEL), BF16)
        XT = [AR.alloc((D_MODEL,), F32) for _ in range(4)]
        JUNK = AR.alloc((D_MODEL,), BF16)
        UB = [AR.alloc((D_MODEL,), BF16) for _ in range(2)]
        UTd = [AR.alloc((8, 256), BF16) for _ in range(2)]
        SG = [AR.alloc((256,), F32) for _ in range(2)]
        FT = AR.alloc((NFC, 256), BF16)
        TMP = AR.alloc((D_MODEL,), F32)
        HO = [AR.alloc((D_MODEL,), F32) for _ in range(1)]
        FGRP = [(0, 768), (768, 1536), (1536, 2176), (2176, 2816)]

        def fgrp(m):
            c0 = m * 128
            for gi_, (a, b) in enumerate(FGRP):
                if a <= c0 < b:
                    return gi_

        for gi_, (a, b) in enumerate(FGRP):
            for k in range(8):
                DMA('pool', ('wg', gi_), [], [('Wg', gi_)], out=Wg[:, k, a:b], in_=w_g[k * 128:(k + 1) * 128, a:b])
            for k in range(8):
                DMA('pool', ('wu', gi_), [], [('Wu', gi_)], out=Wu[:, k, a:b], in_=w_u[k * 128:(k + 1) * 128, a:b])
        load_w(Wd, w_d, 'wd', 'Wd', NFC, D_MODEL)
        NT2 = SOWN // 256
        tile_bufs = {}

        def prepA_D(t):
            tile_bufs[t] = []
            for s in range(2):
                r0 = (t * 2 + s) * 128
                tile_bufs[t].append(prep_a(h1s[r0:r0 + 128, :], 128, XT, UB, JUNK, src_reads=[('h1s', t * 2 + s)]))

        def prepB_D(t):
            for s in range(2):
                prep_b(tile_bufs[t][s][1], 128, 8, UTd[t % 2][:, :, s * 128:(s + 1) * 128], ('utd', t % 2), UB)

        prepA_D(0)
        prepB_D(0)
        for t in range(NT2):
            utd = UTd[t % 2]
            if t + 1 < NT2:
                prepA_D(t + 1)
            for m in range(NFC):
                bg = 2 + (m % 2)
                bu = 4 + (m % 2)
                for c in range(8):
                    OP('pe', 'matmul', [('Wg', fgrp(m)), ('utd', t % 2)], [('ps', bg)], out=psbank(bg)[:, 0:256],
                       lhsT=Wg[:, c, m * 128:(m + 1) * 128], rhs=utd[:, c, :], start=(c == 0), stop=(c == 7))
                sg = SG[m % 2]
                OP('act', 'activation', [('ps', bg)], [('sg', m % 2)], out=sg, in_=psbank(bg)[:, 0:256], func=AF.Silu)
                for c in range(8):
                    OP('pe', 'matmul', [('Wu', fgrp(m)), ('utd', t % 2)], [('ps', bu)], out=psbank(bu)[:, 0:256],
                       lhsT=Wu[:, c, m * 128:(m + 1) * 128], rhs=utd[:, c, :], start=(c == 0), stop=(c == 7))
                OP('dve', 'tensor_tensor', [('ps', bu), ('sg', m % 2)], ['FT'], out=FT[:, m, :], in0=psbank(bu)[:, 0:256],
                   in1=sg, op=ALU.mult)
            for s in range(2):
                b0 = 2 if s == 0 else 6
                for n in range(2):
                    for m in range(NFC):
                        OP('pe', 'matmul', ['FT', 'Wd'], [('ps', b0 + n)], out=psbank(b0 + n),
                           lhsT=FT[:, m, s * 128:(s + 1) * 128], rhs=Wd[:, m, n * 512:(n + 1) * 512],
                           start=(m == 0), stop=(m == NFC - 1))
                if s == 1 and t + 1 < NT2:
                    prepB_D(t + 1)
                hi = 0
                xb = tile_bufs[t][s][0]
                post_norm_residual(b0, XT[xb], ('xt', xb), 1, HO[hi], ('ho', hi), JUNK, TMP)
                o0 = (t * 2 + s) * 128
                DMA('sp', ('hos', hi), [('ho', hi)], [('outd', t * 2 + s)], final=True, out=outd[o0:o0 + 128, :], in_=HO[hi])

        S.emit()
    return nc


_PROGRAM_CACHE = {}


def make_core_inputs(x, w_in, pool_mix, pool_scale, w_branch_a, lam_q1, lam_k1, lam_q2, lam_k2,
                     subln_g, w_branch_b, w_out, mix_pre_g, mix_post_g, ffn_pre_g, ffn_post_g,
                     w_ffn_gate, w_ffn_up, w_ffn_down):
    B, S_LEN, _ = x.shape
    NT = S_LEN // 512
    t0, t1 = own_tiles(NT)
    f = np.float32
    gcols = np.concatenate([
        np.asarray(mix_pre_g[0], f).reshape(8, 128).T,
        np.asarray(ffn_pre_g[0], f).reshape(8, 128).T,
        np.asarray(pool_scale[0], f).reshape(4, 128).T,
        np.asarray(subln_g[0], f).reshape(1, 128).T,
    ], axis=1)
    grows = np.stack([np.asarray(mix_post_g[0], f), np.asarray(ffn_post_g[0], f)], axis=0)
    lamv = np.stack([np.asarray(v[0], f) for v in (lam_q1, lam_k1, lam_q2, lam_k2)], axis=0)
    shared = {
        "w_in": np.ascontiguousarray(w_in[0], f), "pool_mix": np.ascontiguousarray(pool_mix[0], f),
        "w_a": np.ascontiguousarray(w_branch_a[0], f), "w_b": np.ascontiguousarray(w_branch_b[0], f),
        "w_out": np.ascontiguousarray(w_out[0], f), "w_g": np.ascontiguousarray(w_ffn_gate[0], f),
        "w_u": np.ascontiguousarray(w_ffn_up[0], f), "w_d": np.ascontiguousarray(w_ffn_down[0], f),
        "gcols": np.ascontiguousarray(gcols), "grows": np.ascontiguousarray(grows), "lamv": np.ascontiguousarray(lamv),
    }
    in_maps = []
    orders = []
    for core in range(2 * B):
        b, r = core // 2, core % 2
        own = t0 if r == 0 else t1
        oth = t1 if r == 0 else t0
        order = []
        for m in range(NT // 2):
            order += [own[m], oth[m]]
        xb = np.asarray(x[b], f).reshape(NT, 512, D_MODEL)
        xfp = np.ascontiguousarray(xb[order].reshape(S_LEN, D_MODEL))
        xhalo = np.zeros((NT // 2, 16, D_MODEL), f)
        for jj, t in enumerate(own):
            if t > 0:
                xhalo[jj] = x[b, t * 512 - 16:t * 512]
        ksv = np.tile(np.asarray([512.0 * t for t in order], f)[None, :], (128, 1))
        m = dict(shared)
        m.update({"xfp": xfp, "xh": np.ascontiguousarray(xhalo.reshape(-1, D_MODEL)), "ks": np.ascontiguousarray(ksv)})
        in_maps.append(m)
        orders.append(own)
    return in_maps, orders


def kernel(**inputs):
    inputs = {k: np.asarray(v) for k, v in inputs.items()}
    x = inputs["x"]
    B, S_LEN, _ = x.shape
    NT = S_LEN // 512
    if NT not in _PROGRAM_CACHE:
        _PROGRAM_CACHE[NT] = build_program(NT)
    nc = _PROGRAM_CACHE[NT]
    in_maps, orders = make_core_inputs(**inputs)
    res = run_bass_kernel_spmd(nc, in_maps, core_ids=list(range(2 * B)))
    out = np.empty((B, S_LEN, D_MODEL), np.float32)
    for core in range(2 * B):
        b = core // 2
        o = np.asarray(res.results[core]["out"]).reshape(NT // 2, 512, D_MODEL)
        for jj, t in enumerate(orders[core]):
            out[b, t * 512:(t + 1) * 512] = o[jj]
    return out
```
